# Optimizing a Trainium2 kernel written in Bass

```python
import jax, jax.numpy as jnp
from jax import lax
import numpy as np

D_MODEL = 1024
BATCH = 2
SEQ = 16384
DEPTH = 4

N_MIXERS = 2
N_HEADS = 16
HEAD_DIM = D_MODEL // N_HEADS
Q_BLOCK = 128
CONV_WIDTH = 3
D_FF = 4 * D_MODEL
PLE_DIM = 256
N_ATTN = (DEPTH + 1) // 2
N_CONV = DEPTH // 2
RMS_EPS = 1e-6
NEG_INF = -1e30

kernel_name = "fox_shortconv_hybrid_trunk"


def rmsnorm(x, g):
    xf = x.astype(jnp.float32)
    y = xf * lax.rsqrt(jnp.mean(xf * xf, axis=-1, keepdims=True) + RMS_EPS)
    return (y * g.astype(jnp.float32)).astype(x.dtype)


def fox_attention(h, w_in, b_f, w_out):
    B, S, D = h.shape
    proj = h @ w_in
    q = proj[..., :D].reshape(B, S, N_HEADS, HEAD_DIM).transpose(0, 2, 1, 3)
    k = proj[..., D:2 * D].reshape(B, S, N_HEADS, HEAD_DIM).transpose(0, 2, 1, 3)
    v = proj[..., 2 * D:3 * D].reshape(B, S, N_HEADS, HEAD_DIM).transpose(0, 2, 1, 3)
    q = q * jnp.asarray(HEAD_DIM ** -0.5, q.dtype)
    f_logit = (proj[..., 3 * D:] + b_f).astype(jnp.float32)
    log_f = jax.nn.log_sigmoid(f_logit)
    c = lax.cumsum(log_f, axis=1).transpose(0, 2, 1)
    k_pos = jnp.arange(S)

    def q_block(i):
        start = i * Q_BLOCK
        qb = lax.dynamic_slice_in_dim(q, start, Q_BLOCK, axis=2)
        cb = lax.dynamic_slice_in_dim(c, start, Q_BLOCK, axis=2)
        s = jnp.einsum('bhqd,bhkd->bhqk', qb, k, preferred_element_type=jnp.float32)
        s = s + cb[..., :, None] - c[..., None, :]
        q_pos = start + jnp.arange(Q_BLOCK)
        s = jnp.where(k_pos[None, :] <= q_pos[:, None], s, NEG_INF)
        pr = jax.nn.softmax(s, axis=-1)
        return jnp.einsum('bhqk,bhkd->bhqd', pr.astype(v.dtype), v)

    o = lax.map(q_block, jnp.arange(S // Q_BLOCK))
    o = o.transpose(1, 0, 3, 2, 4).reshape(B, S, D)
    return o @ w_out


def short_conv(h, w_in, conv_w, w_out):
    D = h.shape[-1]
    proj = h @ w_in
    b_gate = proj[..., :D]
    c_gate = proj[..., D:2 * D]
    u = proj[..., 2 * D:]
    z = c_gate * u
    zc = lax.conv_general_dilated(
        z, conv_w[:, None, :].astype(z.dtype), window_strides=(1,),
        padding=[(CONV_WIDTH - 1, 0)],
        dimension_numbers=('NWC', 'WIO', 'NWC'),
        feature_group_count=D)
    return (b_gate * zc) @ w_out


def sq_relu_mlp(h, w_up, w_down):
    return jnp.square(jax.nn.relu(h @ w_up)) @ w_down


def setup_inputs(seed: int = 0) -> dict:
    key = jax.random.key(seed)
    ks = jax.random.split(key, 14)
    f32 = jnp.float32
    nrm = lambda k, shape, scale: jax.random.normal(k, shape, f32) * scale
    x = jax.random.normal(ks[0], (BATCH, SEQ, D_MODEL), f32)
    p = jax.random.normal(ks[1], (DEPTH, BATCH, SEQ, PLE_DIM), f32)
    norm_g = 1.0 + nrm(ks[2], (DEPTH, 6, D_MODEL), 0.05)
    w_attn_in = nrm(ks[3], (N_ATTN, D_MODEL, 3 * D_MODEL + N_HEADS), D_MODEL ** -0.5)
    b_forget = 2.0 + nrm(ks[4], (N_ATTN, N_HEADS), 0.5)
    w_attn_out = nrm(ks[5], (N_ATTN, D_MODEL, D_MODEL), D_MODEL ** -0.5)
    w_conv_in = nrm(ks[6], (N_CONV, D_MODEL, 3 * D_MODEL), D_MODEL ** -0.5)
    conv_w = nrm(ks[7], (N_CONV, CONV_WIDTH, D_MODEL), CONV_WIDTH ** -0.5)
    w_conv_out = nrm(ks[8], (N_CONV, D_MODEL, D_MODEL), D_MODEL ** -0.5)
    w_mlp_up = nrm(ks[9], (DEPTH, D_MODEL, D_FF), D_MODEL ** -0.5)
    w_mlp_down = nrm(ks[10], (DEPTH, D_FF, D_MODEL), D_FF ** -0.5)
    w_ple_proj = nrm(ks[11], (DEPTH, PLE_DIM, D_MODEL), PLE_DIM ** -0.5)
    w_ple_gate = nrm(ks[12], (DEPTH, D_MODEL, D_MODEL), D_MODEL ** -0.5)
    return {"x": x, "p": p, "norm_g": norm_g, "w_attn_in": w_attn_in,
            "b_forget": b_forget, "w_attn_out": w_attn_out, "w_conv_in": w_conv_in,
            "conv_w": conv_w, "w_conv_out": w_conv_out, "w_mlp_up": w_mlp_up,
            "w_mlp_down": w_mlp_down, "w_ple_proj": w_ple_proj, "w_ple_gate": w_ple_gate}


def reference(x, p, norm_g, w_attn_in, b_forget, w_attn_out, w_conv_in, conv_w,
              w_conv_out, w_mlp_up, w_mlp_down, w_ple_proj, w_ple_gate):
    for i in range(DEPTH):
        g = norm_g[i]
        hn = rmsnorm(x, g[0])
        j = i // N_MIXERS
        if i % N_MIXERS == 0:
            m = fox_attention(hn, w_attn_in[j], b_forget[j], w_attn_out[j])
        else:
            m = short_conv(hn, w_conv_in[j], conv_w[j], w_conv_out[j])
        x = x + rmsnorm(m, g[1])
        f = sq_relu_mlp(rmsnorm(x, g[2]), w_mlp_up[i], w_mlp_down[i])
        x = x + rmsnorm(f, g[3])
        gate = jax.nn.sigmoid(rmsnorm(x, g[4]) @ w_ple_gate[i])
        e = (p[i] @ w_ple_proj[i]) * gate
        x = x + rmsnorm(e, g[5])
    return x
```

```python
import numpy as np
from contextlib import ExitStack
import concourse.bass as bass
import concourse.mybir as mybir
from concourse.bass_utils import run_bass_kernel_spmd

F32 = mybir.dt.float32
BF16 = mybir.dt.bfloat16
AF = mybir.ActivationFunctionType
ALU = mybir.AluOpType

D = 1024
H = 16
DH = 64
DFF = 4096
PLE = 256
TS = 512
EPS = 1e-6
NEG = -30000.0
GRP4 = [[0, 1, 2, 3], [4, 5, 6, 7]]
GRP8 = [[0, 1, 2, 3, 4, 5, 6, 7]]


class StopBuild(Exception):
    pass


class Buf:
    __slots__ = ("name", "w", "r")

    def __init__(self, name):
        self.name = name
        self.w = None
        self.r = {}


class Ctx:
    def __init__(self, nc, es):
        self.nc = nc
        self.es = es
        self.engs = {"pe": nc.tensor, "dve": nc.vector, "act": nc.scalar,
                     "pool": nc.gpsimd, "sp": nc.sync}
        self.sems = {}
        self.cnt = {}
        self.waited = {}
        for k in self.engs:
            self._mksem("E_" + k)
        self.ninstr = 0

    def _mksem(self, key):
        self.sems[key] = self.es.enter_context(self.nc.semaphore(key))
        self.cnt[key] = 0

    def _wait(self, engname, ev):
        if ev is None:
            return
        key, val = ev
        wk = (engname, key)
        if self.waited.get(wk, 0) >= val:
            return
        self.engs[engname].wait_ge(self.sems[key], val)
        self.waited[wk] = val

    def op(self, engname, fn, reads=(), writes=(), dma=None, inc=16):
        own = "E_" + engname
        deps = []
        for b in reads:
            if b.w is not None:
                deps.append(b.w)
        for b in writes:
            if b.w is not None:
                deps.append(b.w)
            deps.extend(b.r.items())
        for ev in deps:
            if dma is None and ev[0] == own and engname == "pe":
                continue
            self._wait(engname, ev)
        ins = fn()
        if dma is None:
            self.cnt[own] += 1
            ev = (own, self.cnt[own])
            ins.then_inc(self.sems[own], 1)
        else:
            if dma not in self.sems:
                self._mksem(dma)
            self.cnt[dma] += inc
            ev = (dma, self.cnt[dma])
            ins.then_inc(self.sems[dma], inc)
        for b in reads:
            if b.r.get(ev[0], 0) < ev[1]:
                b.r[ev[0]] = ev[1]
        for b in writes:
            b.w = ev
            b.r = {}
        self.ninstr += 1
        return ev


def build(NT, phase, stop=None):
    SL = NT * TS
    NB = 16 * NT
    NGT = 4 * NT
    nc = bass.Bass("TRN2", target_bir_lowering=False)
    dt = nc.dram_tensor

    def ext(name, shape, dtype=F32):
        return dt(name, list(shape), dtype, kind="ExternalInput").ap()

    def exto(name, shape, dtype=F32):
        return dt(name, list(shape), dtype, kind="ExternalOutput").ap()

    has_pre = phase in ("pre", "convpre")
    has_attn = phase == "attn"
    has_conv = phase in ("convpre", "conv")
    LA = 0
    LC = 1
    x_in = ext("x", [SL, D])
    ng_by_L = {}
    wsrc = {}
    if has_attn or has_conv:
        p_in = ext("p", [SL, PLE])
        Lm = LA if has_attn else LC
        ng_by_L[Lm] = ext("norm_g", [6, D])
        wsrc["aout%d" % Lm if has_attn else "cout%d" % Lm] = ext("w_out", [D, D])
        wsrc["up%d" % Lm] = ext("w_up", [D, DFF])
        wsrc["dn%d" % Lm] = ext("w_dn", [DFF, D])
        wsrc["pp%d" % Lm] = ext("w_pp", [PLE, D])
        wsrc["pg%d" % Lm] = ext("w_pg", [D, D])
        out_d = exto("out", [SL, D])
    if has_conv:
        cw_in = ext("conv_w", [3, D])
        wsrc["cin%d" % LC] = ext("w_cin", [D, 3 * D])
        xhall = ext("xhall", [8 * NT, D])
    if has_pre:
        LP = 0 if phase == "pre" else 2
        ng_by_L[LP] = ext("norm_g_pre", [6, D])
        bf_in = ext("b_forget", [H])
        wsrc["ain%d" % LP] = ext("w_ain", [D, 3 * D + H])
        qbuf = exto("qbuf", [NT * D, TS], BF16)
        kloc = exto("kloc", [NT * D, TS], BF16)
        vloc = exto("vloc", [NT * TS, H * 65], BF16)
        lfloc = exto("lfloc", [128, NT * 64])
    if has_attn:
        qbuf = ext("qbuf", [NT * D, TS], BF16)
        kloc = ext("kloc", [NT * D, TS], BF16)
        vloc = ext("vloc", [NT * TS, H * 65], BF16)
        kall = ext("kall", [4 * NT * D, TS], BF16)
        vall = ext("vall", [4 * NT * TS, H * 65], BF16)
        lfall = ext("lfall", [512, NT * 64])
        xhloc = exto("xhloc", [2 * NT, D])
        cqd = dt("cqd", [2, NT, H, TS], BF16).ap()
    maddb_in = ext("maddb", [128, NT * NB])
    sel_in = ext("sel", [128, 16])
    selh_in = ext("selh", [128, 128])
    b_qbuf = [Buf("qbuf%d" % m) for m in range(NT)]
    b_cqd = [Buf("cqd%d" % m) for m in range(NT)]
    b_kloc, b_kall, b_vloc, b_vall = Buf("kloc"), Buf("kall"), Buf("vloc"), Buf("vall")
    b_lfloc, b_lfall, b_xhloc, b_xhall = Buf("lfloc"), Buf("lfall"), Buf("xhloc"), Buf("xhall")

    es = ExitStack()
    c = Ctx(nc, es)

    def sb(name, shape, dtype):
        t = es.enter_context(nc.sbuf_tensor(name, list(shape), dtype))
        import os as _o
        if _o.environ.get("KPRINT"):
            print("SBUF", name, shape, t[:].tensor)
        return t

    def fence(buf):
        c.op("pool", lambda: nc.gpsimd.memset(fdum[:], 0.0), reads=[buf], writes=[buf, b_fdum])

    fdum = sb("fdum", [128, 8], F32); b_fdum = Buf("fdum")
    xt = sb("xt", [128, 4, D], F32); b_xt = [Buf("xt%d" % s) for s in range(4)]
    Tt = [sb("T%d" % i, [128, D], F32) for i in range(3)]; b_T = [Buf("T%d" % i) for i in range(3)]
    junk = sb("junk", [128, D], BF16); b_junk = Buf("junk")
    hnT = sb("hnT", [128, 8, TS], BF16); b_hnT = Buf("hnT")
    NSLOT = 4
    wslot = [sb("ws%d" % i, [128, 4096], BF16) for i in range(NSLOT)]
    b_ws = [Buf("ws%d" % i) for i in range(NSLOT)]
    arena = sb("arena", [128, 32, TS], BF16); b_ar = [Buf("ar%d" % i) for i in range(32)]
    ktile = [sb("kt%d" % i, [66, 4, TS], BF16) for i in range(2)]; b_kt = [Buf("kt%d" % i) for i in range(2)]
    vtile = [sb("vt%d" % i, [128, 4, 4 * 65], BF16) for i in range(2)]; b_vt = [Buf("vt%d" % i) for i in range(2)]
    NPT = 3
    ptile = [sb("pt%d" % i, [128, TS], BF16) for i in range(NPT)]; b_pt = [Buf("pt%d" % i) for i in range(NPT)]
    negc = sb("negc", [128, NB, H], F32); b_negc = Buf("negc")
    nown = sb("nown", [128, NT, 4 * H], F32); b_nown = Buf("nown")
    biasm = sb("biasm", [128, 4, NB], F32); b_biasm = Buf("biasm")
    maddb = sb("maddb_s", [128, NT * NB], F32)
    selt = sb("sel_s", [128, 16], F32); b_sel = Buf("sel"); b_maddb = Buf("maddb")
    selh = sb("selh_s", [128, 128], F32); b_selh = Buf("selh")
    trim = sb("trim", [128, 128], BF16); b_trim = Buf("trim")
    trimf = sb("trimf", [128, 128], F32); b_trimf = Buf("trimf")
    ident = sb("ident", [128, 128], F32); b_ident = Buf("ident")
    identb = sb("identb", [128, 128], BF16); b_identb = Buf("identb")
    ones_f = sb("ones_f", [128, 128], F32); b_ones = Buf("ones")
    OTs = Tt[2][0:65, 0:TS]; b_OTs = b_T[2]
    rinv = Tt[2][:, TS:2 * TS]; b_rinv = b_T[2]
    tric = sb("tric", [128, 128], F32); b_tric = Buf("tric")
    tricb = sb("tricb", [128, 128], BF16); b_tricb = Buf("tricb")
    onesb = sb("onesb", [128, 128], BF16); b_onesb = Buf("onesb")
    selhb = sb("selhb", [128, 128], BF16); b_selhb = Buf("selhb")
    cTt = [sb("cT%d" % i, [128, 128], BF16) for i in range(2)]; b_cT = [Buf("cT%d" % i) for i in range(2)]
    import os as _os0
    if _os0.environ.get("KDBG") == "lfcown":
        _lfc = sb("lfcx", [16, 2 * TS], F32); b_lfcx = Buf("lfcx")
        lfc = [_lfc[:, i * TS:(i + 1) * TS] for i in range(2)]; b_lfc = [b_lfcx, b_lfcx]
    elif _os0.environ.get("KDBG") == "lfcT2":
        lfc = [Tt[2][0:16, i * TS:(i + 1) * TS] for i in range(2)]; b_lfc = [b_T[2], b_T[2]]
    elif _os0.environ.get("KDBG") == "lfcxt":
        lfc = [xt[0:16, 0, i * TS:(i + 1) * TS] for i in range(2)]; b_lfc = [b_xt[0], b_xt[0]]
    else:
        lfc = [Tt[0][0:16, i * TS:(i + 1) * TS] for i in range(2)]; b_lfc = [b_T[0], b_T[0]]
    cc = [Tt[1][0:16, i * TS:(i + 1) * TS] for i in range(2)]; b_cc = [b_T[1], b_T[1]]
    ccp = [Tt[1][:, i * TS:(i + 1) * TS] for i in range(2)]
    cqacc = Tt[2][0:16, 0:TS]; b_cqacc = b_T[2]
    vst = sb("vst", [128, 4, H, 65], BF16); b_vst = Buf("vst")
    qkst = [sb("qkst%d" % i, [128, 2, TS], BF16) for i in range(2)]; b_qkst = [Buf("qkst%d" % i) for i in range(2)]
    csb = Tt[0][:, 0:TS]; b_csb = b_T[0]
    zc = Tt[0][:, TS:2 * TS]; b_zc = b_T[0]
    zt = Tt[1][:, 0:TS + 2]; b_zt = b_T[1]
    zh = sb("zh", [128, 8, 2 * NT], F32); b_zh = Buf("zh")
    hnTh = sb("hnTh", [128, 8, 2 * NT], BF16); b_hnTh = Buf("hnTh")
    chh = sb("chh", [128, 2 * NT], F32); b_chh = Buf("chh")
    gbc = [sb("gbc%d" % i, [128, D], F32) for i in range(2)]; b_gbc = [Buf("gbc%d" % i) for i in range(2)]
    gsrc = sb("gsrc", [128, 128], F32); b_gsrc = Buf("gsrc")
    gTT = [sb("gTT%d" % i, [128, 128], F32) for i in range(2)]; b_gTT = [Buf("gTT%d" % i) for i in range(2)]
    bfb = sb("bfb", [128, 4, H], F32); b_bfb = Buf("bfb")
    ptl = sb("ptl", [128, 4, PLE], F32); b_ptl = Buf("ptl")
    pT = sb("pT", [128, 2, TS], BF16); b_pT = Buf("pT")
    ss = sb("ss", [128, 8], F32); b_ss = Buf("ss")
    rstd = sb("rstd", [128, 8], F32); b_rstd = Buf("rstd")
    fz = sb("fz", [128, 4, H], F32); b_fz = Buf("fz")
    fa = sb("fa", [128, 4, H], F32); b_fa = Buf("fa")
    fl = sb("fl", [128, 4, H], F32); b_fl = Buf("fl")
    lfst = Tt[2][0:16, 0:TS]; b_lfst = b_T[2]

    ps = [es.enter_context(nc.psum_tensor("ps%d" % i, [128, TS], F32)) for i in range(8)]
    b_ps = [Buf("ps%d" % i) for i in range(8)]
    st = {"bank": 0, "ws": 0, "g": 0}

    def nbank(lo=0, hi=8):
        b = st["bank"]
        if b < lo or b >= hi:
            b = lo
        st["bank"] = b + 1 if b + 1 < hi else lo
        return b

    def wload(dram_view, shape):
        i = st["ws"]; st["ws"] = (i + 1) % NSLOT
        n = int(np.prod(shape[1:]))
        v = wslot[i][0:shape[0], 0:n]
        if len(shape) == 3:
            v = v.rearrange("p (a b) -> p a b", a=shape[1])
        c.op("sp", lambda: nc.sync.dma_start(out=v, in_=dram_view), reads=[], writes=[b_ws[i]],
             dma="ws%d" % i)
        return v, b_ws[i]

    T2b = Tt[2][:].bitcast(BF16)

    def split3(src, src_bufs, n):
        h1 = junk[:, 0:n]; h2 = junk[:, TS:TS + n]; h3 = T2b[:, 0:n]
        r = Tt[2][:, TS:TS + n]
        c.op("dve", lambda: nc.vector.tensor_copy(out=h1, in_=src), reads=src_bufs, writes=[b_junk])
        c.op("dve", lambda: nc.vector.tensor_tensor(out=r, in0=src, in1=h1, op=ALU.subtract),
             reads=src_bufs + [b_junk], writes=[b_T[2]])
        c.op("dve", lambda: nc.vector.tensor_copy(out=h2, in_=r), reads=[b_T[2]], writes=[b_junk])
        c.op("dve", lambda: nc.vector.tensor_tensor(out=r, in0=r, in1=h2, op=ALU.subtract),
             reads=[b_T[2], b_junk], writes=[b_T[2]])
        c.op("dve", lambda: nc.vector.tensor_copy(out=h3, in_=r), reads=[b_T[2]], writes=[b_T[2]])
        return [(h1, b_junk), (h2, b_junk), (h3, b_T[2])]

    c.op("pool", lambda: nc.gpsimd.memset(ident[:], 0.0), writes=[b_ident])
    c.op("pool", lambda: nc.gpsimd.affine_select(out=ident[:], in_=ident[:], pattern=[[-1, 128]],
                                                  compare_op=ALU.not_equal, fill=1.0, base=0,
                                                  channel_multiplier=1), reads=[b_ident], writes=[b_ident])
    c.op("pool", lambda: nc.gpsimd.tensor_copy(out=identb[:], in_=ident[:]), reads=[b_ident], writes=[b_identb])
    c.op("pool", lambda: nc.gpsimd.memset(trimf[:], 0.0), writes=[b_trimf])
    c.op("pool", lambda: nc.gpsimd.affine_select(out=trimf[:], in_=trimf[:], pattern=[[1, 128]],
                                                  compare_op=ALU.is_ge, fill=NEG, base=0,
                                                  channel_multiplier=-1), reads=[b_trimf], writes=[b_trimf])
    c.op("pool", lambda: nc.gpsimd.tensor_copy(out=trim[:], in_=trimf[:]), reads=[b_trimf], writes=[b_trim])
    c.op("pool", lambda: nc.gpsimd.memset(ones_f[:], 1.0), writes=[b_ones])
    c.op("pool", lambda: nc.gpsimd.memset(gsrc[:], 0.0), writes=[b_gsrc])
    for i in range(NSLOT):
        c.op("pool", lambda: nc.gpsimd.memset(wslot[i][:], 0.0), writes=[b_ws[i]])
    c.op("pool", lambda: nc.gpsimd.memset(tric[:], 1.0), writes=[b_tric])
    c.op("pool", lambda: nc.gpsimd.affine_select(out=tric[:], in_=tric[:], pattern=[[1, 128]],
                                                  compare_op=ALU.is_ge, fill=0.0, base=0,
                                                  channel_multiplier=-1), reads=[b_tric], writes=[b_tric])
    c.op("pool", lambda: nc.gpsimd.memset(vst[:].rearrange("p s h e -> p (s h e)"), 1.0), writes=[b_vst])
    for i in range(2):
        c.op("pool", lambda: nc.gpsimd.memset(ktile[i][:].rearrange("p a b -> p (a b)"), 1.0), writes=[b_kt[i]])
    c.op("pool", lambda: nc.gpsimd.tensor_copy(out=tricb[:], in_=tric[:]), reads=[b_tric], writes=[b_tricb])
    c.op("pool", lambda: nc.gpsimd.memset(onesb[:], 1.0), writes=[b_onesb])
    c.op("sp", lambda: nc.sync.dma_start(out=maddb[:], in_=maddb_in[:, :]), writes=[b_maddb], dma="c0")
    c.op("sp", lambda: nc.sync.dma_start(out=selt[:], in_=sel_in[:, :]), writes=[b_sel], dma="c1")
    sel = selt[:, 0:4]
    c.op("sp", lambda: nc.sync.dma_start(out=selh[:], in_=selh_in[:, :]), writes=[b_selh], dma="c2")
    c.op("pool", lambda: nc.gpsimd.tensor_copy(out=selhb[:], in_=selh[:]), reads=[b_selh], writes=[b_selhb])

    wfull = {}
    b_wf = {}

    def cast_weights():
        for key, src in wsrc.items():
            rows, cols = src.shape
            wf = dt("wf_" + key, [rows, cols], BF16).ap()
            bf_ = Buf("wf_" + key)
            for r0 in range(0, rows, 128):
                r1 = min(rows, r0 + 128)
                c.op("pool", lambda: nc.gpsimd.dma_start(out=wf[r0:r1, :], in_=src[r0:r1, :], max_dma_last_dim=4096),
                     writes=[bf_], dma="wc_" + key)
            wfull[key] = wf
            b_wf[key] = bf_

    def wcols(key, c0, ncols, kp=128):
        wf = wfull[key]
        v, b = wload(wf[:, c0:c0 + ncols].rearrange("(k p) n -> p k n", p=kp), [kp, wf.shape[0] // kp, ncols])
        return v, b

    def wl_dep(key):
        return [b_wf[key]]

    _wload_plain = wload

    def wload_dep(dram_view, shape, deps):
        i = st["ws"]; st["ws"] = (i + 1) % NSLOT
        n = int(np.prod(shape[1:]))
        v = wslot[i][0:shape[0], 0:n]
        vfull = wslot[i][:, 0:n]
        if len(shape) == 3:
            v = v.rearrange("p (a b) -> p a b", a=shape[1])
            vfull = vfull.rearrange("p (a b) -> p a b", a=shape[1])
        c.op("sp", lambda: nc.sync.dma_start(out=v, in_=dram_view), reads=deps, writes=[b_ws[i]],
             dma="ws%d" % i)
        return vfull, b_ws[i]

    def wblock(key, r0, nrows, c0, ncols, kp=128):
        wf = wfull[key]
        view = wf[r0:r0 + nrows, c0:c0 + ncols].rearrange("(k p) n -> p k n", p=kp)
        return wload_dep(view, [kp, nrows // kp, ncols], [b_wf[key]])

    def load_layer_consts(L):
        par = L % 2
        c.op("sp", lambda: nc.sync.dma_start(out=gsrc[0:48, :], in_=ng_by_L[L].rearrange("g (c p) -> (g c) p", p=128)),
             writes=[b_gsrc], dma="gsrc")
        if par == 1:
            c.op("sp", lambda: nc.sync.dma_start(out=gsrc[48:72, :], in_=cw_in.rearrange("k (c p) -> (k c) p", p=128)),
                 writes=[b_gsrc], dma="gsrc2")
        elif has_pre:
            c.op("sp", lambda: nc.sync.dma_start(out=bfb[:, 0, :], in_=bf_in.partition_broadcast(128)),
                 writes=[b_bfb], dma="bfb")
            for s_ in range(1, 4):
                c.op("pool", lambda: nc.gpsimd.tensor_copy(out=bfb[:, s_, :], in_=bfb[:, 0, :]), reads=[b_bfb], writes=[b_bfb])
        bk = nbank()
        c.op("pe", lambda: nc.tensor.transpose(out=ps[bk][:, 0:128], in_=gsrc[:, :], identity=ident[:]),
             reads=[b_gsrc, b_ident], writes=[b_ps[bk]])
        c.op("act", lambda: nc.scalar.copy(out=gTT[par][:], in_=ps[bk][:, 0:128]), reads=[b_ps[bk]], writes=[b_gTT[par]])

    def load_gbc(L, gi):
        i = st["g"]; st["g"] = 1 - i
        c.op("sp", lambda: nc.sync.dma_start(out=gbc[i][:], in_=ng_by_L[L][gi, :].partition_broadcast(128)),
             writes=[b_gbc[i]], dma="gbc%d" % i)
        return gbc[i], b_gbc[i]

    def load_x(m, src, src_bufs):
        import os as _o
        if _o.environ.get("KDBG") == "xnowar":
            c.op("sp", lambda: nc.sync.dma_start(out=xt[:], in_=src[m * TS:(m + 1) * TS, :].rearrange("(s p) d -> p s d", p=128)),
                 reads=src_bufs, writes=[], dma="xt")
            return
        c.op("sp", lambda: nc.sync.dma_start(out=xt[:], in_=src[m * TS:(m + 1) * TS, :].rearrange("(s p) d -> p s d", p=128)),
             reads=src_bufs, writes=b_xt, dma="xt")

    def store_x(m, dst, dst_bufs, halo):
        c.op("pool", lambda: nc.gpsimd.dma_start(out=dst[m * TS:(m + 1) * TS, :].rearrange("(s p) d -> p s d", p=128), in_=xt[:]),
             reads=b_xt, writes=dst_bufs, dma="xst")
        if halo:
            c.op("pool", lambda: nc.gpsimd.dma_start(out=xhloc[2 * m:2 * m + 2, :], in_=xt[126:128, 3, :]),
                 reads=[b_xt[3]], writes=[b_xhloc], dma="xhst")

    def rstd_from_ss(n, col0=0):
        c.op("act", lambda: nc.scalar.activation(out=rstd[:, col0:col0 + n], in_=ss[:, col0:col0 + n], func=AF.Sqrt,
                                                 scale=1.0 / D, bias=EPS), reads=[b_ss], writes=[b_rstd])
        c.op("dve", lambda: nc.vector.reciprocal(out=rstd[:, col0:col0 + n], in_=rstd[:, col0:col0 + n]),
             reads=[b_rstd], writes=[b_rstd])

    def norm_to_T(gi, L):
        for s in range(4):
            c.op("act", lambda: nc.scalar.activation(out=junk[:], in_=xt[:, s, :], func=AF.Square,
                                                     accum_out=ss[:, s:s + 1]),
                 reads=[b_xt[s]], writes=[b_junk, b_ss])
        rstd_from_ss(4)
        for s in range(4):
            ti = s % 2
            c.op("act", lambda: nc.scalar.activation(out=Tt[ti][:], in_=xt[:, s, :], func=AF.Copy,
                                                     scale=rstd[:, s:s + 1]),
                 reads=[b_xt[s], b_rstd], writes=[b_T[ti]])
            for half in range(2):
                bk = nbank()
                for kk in range(4):
                    k = half * 4 + kk
                    c.op("pe", lambda: nc.tensor.transpose(out=ps[bk][:, kk * 128:(kk + 1) * 128],
                                                           in_=Tt[ti][:, k * 128:(k + 1) * 128], identity=ident[:]),
                         reads=[b_T[ti], b_ident], writes=[b_ps[bk]])
                for kk in range(4):
                    k = half * 4 + kk
                    c.op("dve", lambda: nc.vector.tensor_scalar(out=hnT[:, k, s * 128:(s + 1) * 128],
                                                                in0=ps[bk][:, kk * 128:(kk + 1) * 128],
                                                                scalar1=gTT[L % 2][:, gi * 16 + k:gi * 16 + k + 1], scalar2=None, op0=ALU.mult),
                         reads=[b_ps[bk], b_gTT[L % 2]], writes=[b_hnT])

    def res_norm(s, srcs, src_bufs, gtile, gbuf):
        for h2 in range(2):
            c.op("act", lambda: nc.scalar.activation(out=junk[:, 0:TS], in_=srcs[h2], func=AF.Square,
                                                     accum_out=ss[:, 4 + h2:5 + h2]),
                 reads=[src_bufs[h2]], writes=[b_junk, b_ss])
        c.op("dve", lambda: nc.vector.tensor_add(out=ss[:, 6:7], in0=ss[:, 4:5], in1=ss[:, 5:6]),
             reads=[b_ss], writes=[b_ss])
        rstd_from_ss(1, 6)
        for h2 in range(2):
            c.op("dve", lambda: nc.vector.scalar_tensor_tensor(out=Tt[2][:, h2 * TS:(h2 + 1) * TS], in0=srcs[h2],
                                                               scalar=rstd[:, 6:7], in1=gtile[:, h2 * TS:(h2 + 1) * TS],
                                                               op0=ALU.mult, op1=ALU.mult),
                 reads=[src_bufs[h2], b_rstd, gbuf], writes=[b_T[2]])
        c.op("pool", lambda: nc.gpsimd.tensor_tensor(out=xt[:, s, :], in0=xt[:, s, :], in1=Tt[2][:], op=ALU.add),
             reads=[b_xt[s], b_T[2]], writes=[b_xt[s]])

    def stage_attn_pre(L, m):
        key = "ain%d" % L
        norm_to_T(0, L)
        if stop == "S0a":
            raise StopBuild()
        nst = 0
        for which in range(2):
            for cb in range(2):
                W, bW = wblock(key, 0, D, which * D + cb * 512, 512)
                for hq in range(2):
                    si = nst % 2; nst += 1
                    for p2 in range(2):
                        pr = hq * 2 + p2
                        bk = nbank()
                        for k in range(8):
                            c.op("pe", lambda: nc.tensor.matmul(ps[bk][:, :], lhsT=W[:, k, pr * 128:(pr + 1) * 128],
                                                                rhs=hnT[:, k, :], start=(k == 0), stop=(k == 7)),
                                 reads=[bW, b_hnT], writes=[b_ps[bk]])
                        if which == 0:
                            c.op("act", lambda: nc.scalar.activation(out=qkst[si][:, p2, :], in_=ps[bk][:, :],
                                                                     func=AF.Copy, scale=0.125),
                                 reads=[b_ps[bk]], writes=[b_qkst[si]])
                        else:
                            c.op("dve", lambda: nc.vector.tensor_copy(out=qkst[si][:, p2, :], in_=ps[bk][:, :]),
                                 reads=[b_ps[bk]], writes=[b_qkst[si]])
                    dstt = qbuf if which == 0 else kloc
                    dbuf = [b_qbuf[m]] if which == 0 else [b_kloc]
                    r0 = m * D + cb * 512 + hq * 256
                    c.op("pool", lambda: nc.gpsimd.dma_start(out=dstt[r0:r0 + 256, :].rearrange("(i q) t -> q i t", q=128),
                                                             in_=qkst[si][:]),
                         reads=[b_qkst[si]], writes=dbuf, dma="qkst%d" % si)
        if stop == "S0b":
            raise StopBuild()
        for cb in range(2):
            W, bW = wblock(key, 0, D, 2 * D + cb * 512, 512)
            for s in range(4):
                bk = nbank()
                for k in range(8):
                    c.op("pe", lambda: nc.tensor.matmul(ps[bk][:, :], lhsT=hnT[:, k, s * 128:(s + 1) * 128],
                                                        rhs=W[:, k, :], start=(k == 0), stop=(k == 7)),
                         reads=[bW, b_hnT], writes=[b_ps[bk]])
                c.op("dve", lambda: nc.vector.tensor_copy(out=vst[:, s, cb * 8:(cb + 1) * 8, 0:64],
                                                          in_=ps[bk][:, :].rearrange("p (h d) -> p h d", d=64)),
                     reads=[b_ps[bk]], writes=[b_vst])
        c.op("pool", lambda: nc.gpsimd.dma_start(out=vloc[m * TS:(m + 1) * TS, :].rearrange("(s p) e -> p s e", p=128),
                                                 in_=vst[:].rearrange("p s h e -> p s (h e)")),
             reads=[b_vst], writes=[b_vloc], dma="vst")
        if stop == "S0c":
            raise StopBuild()
        W, bW = wblock(key, 0, D, 3 * D + H - 512, 512)
        bk = nbank()
        for s in range(4):
            for k in range(8):
                c.op("pe", lambda: nc.tensor.matmul(ps[bk][:, s * H:(s + 1) * H], lhsT=hnT[:, k, s * 128:(s + 1) * 128],
                                                    rhs=W[:, k, 512 - H:512], start=(k == 0), stop=(k == 7)),
                     reads=[bW, b_hnT], writes=[b_ps[bk]])
        f2 = lambda t: t[:].rearrange("p s h -> p (s h)")
        c.op("dve", lambda: nc.vector.tensor_tensor(out=f2(fz), in0=ps[bk][:, 0:4 * H], in1=f2(bfb), op=ALU.add),
             reads=[b_ps[bk], b_bfb], writes=[b_fz])
        c.op("act", lambda: nc.scalar.activation(out=f2(fa), in_=f2(fz), func=AF.Abs),
             reads=[b_fz], writes=[b_fa])
        c.op("act", lambda: nc.scalar.activation(out=f2(fa), in_=f2(fa), func=AF.Exp, scale=-1.0),
             reads=[b_fa], writes=[b_fa])
        c.op("act", lambda: nc.scalar.activation(out=f2(fa), in_=f2(fa), func=AF.Ln, bias=1.0),
             reads=[b_fa], writes=[b_fa])
        c.op("dve", lambda: nc.vector.tensor_scalar_min(out=f2(fl), in0=f2(fz), scalar1=0.0),
             reads=[b_fz], writes=[b_fl])
        c.op("dve", lambda: nc.vector.tensor_sub(out=f2(fl), in0=f2(fl), in1=f2(fa)),
             reads=[b_fl, b_fa], writes=[b_fl])
        c.op("pool", lambda: nc.gpsimd.dma_start(out=lfloc[:, m * 64:(m + 1) * 64], in_=f2(fl)),
             reads=[b_fl], writes=[b_lfloc], dma="lfst")

    def attn_exchange(L):
        if stop == "ex0a":
            raise StopBuild()
        NC16 = NB * H
        af = arena[:].rearrange("p a b -> p (a b)").bitcast(F32)
        Sv = af[:, 0:NC16]
        Pv = af[:, 2048:2048 + NC16]
        LFv = af[:, 4096:4096 + NC16]
        negf = negc[:].rearrange("p b h -> p (b h)")
        c.op("sp", lambda: nc.sync.dma_start(out=LFv.rearrange("p (m r x) -> p m r x", r=4, x=64),
                                             in_=lfall.rearrange("(r p) (m x) -> p m r x", p=128, x=64)),
             reads=[b_lfall], writes=b_ar, dma="lfl")
        if stop == "ex0b":
            raise StopBuild()
        nch = (NC16 + TS - 1) // TS
        for ch in range(nch):
            w = min(TS, NC16 - ch * TS)
            bA = nbank(); bB = nbank()
            terms = split3(LFv[:, ch * TS:ch * TS + w], list(b_ar), w)
            for ti_, (hh_, hb_) in enumerate(terms):
                c.op("pe", lambda: nc.tensor.matmul(ps[bA][:, 0:w], lhsT=tricb[:, :], rhs=hh_,
                                                    start=(ti_ == 0), stop=(ti_ == 2)), reads=[hb_, b_tricb], writes=[b_ps[bA]])
            for ti_, (hh_, hb_) in enumerate(terms):
                c.op("pe", lambda: nc.tensor.matmul(ps[bB][:, 0:w], lhsT=onesb[:, :], rhs=hh_,
                                                    start=(ti_ == 0), stop=(ti_ == 2)), reads=[hb_, b_onesb], writes=[b_ps[bB]])
            c.op("act", lambda: nc.scalar.copy(out=negf[:, ch * TS:ch * TS + w], in_=ps[bA][:, 0:w]),
                 reads=[b_ps[bA]], writes=[b_negc])
            c.op("dve", lambda: nc.vector.tensor_copy(out=Sv[:, ch * TS:ch * TS + w], in_=ps[bB][:, 0:w]),
                 reads=[b_ps[bB]], writes=b_ar)
        if stop == "ex0b2":
            raise StopBuild()
        c.op("dve", lambda: nc.vector.memset(Pv[:, 0:H], 0.0), writes=b_ar)
        for b_ in range(1, NB):
            c.op("dve", lambda: nc.vector.tensor_add(out=Pv[:, b_ * H:(b_ + 1) * H], in0=Pv[:, (b_ - 1) * H:b_ * H],
                                                     in1=Sv[:, (b_ - 1) * H:b_ * H]), reads=b_ar[0:1], writes=b_ar[0:1])
        if stop == "ex0b3":
            raise StopBuild()
        c.op("dve", lambda: nc.vector.scalar_tensor_tensor(out=negf, in0=negf, scalar=-1.0, in1=Pv, op0=ALU.mult,
                                                           op1=ALU.subtract), reads=b_ar + [b_negc], writes=[b_negc])
        if stop == "ex0c":
            raise StopBuild()
        nv = negc[:].rearrange("p (m r k) h -> p m r (k h)", r=4, k=4)
        for r in range(4):
            if r == 0:
                c.op("dve", lambda: nc.vector.tensor_scalar(out=nown[:], in0=nv[:, :, 0, :], scalar1=sel[:, 0:1],
                                                            scalar2=None, op0=ALU.mult),
                     reads=[b_negc, b_sel], writes=[b_nown])
            else:
                c.op("dve", lambda: nc.vector.scalar_tensor_tensor(out=nown[:], in0=nv[:, :, r, :], scalar=sel[:, r:r + 1],
                                                                   in1=nown[:], op0=ALU.mult, op1=ALU.add),
                     reads=[b_negc, b_sel, b_nown], writes=[b_nown])
        ncol = max(128, NT * 64)
        nownf = nown[:].rearrange("p m x -> p (m x)")
        hb = junk[:, 0:ncol]
        hi32 = Tt[0][:, 0:ncol]
        lo32 = Tt[1][:, 0:ncol]
        c.op("pool", lambda: nc.gpsimd.memset(Tt[0][:], 0.0), writes=[b_T[0]])
        c.op("pool", lambda: nc.gpsimd.memset(Tt[1][:], 0.0), writes=[b_T[1]])
        c.op("dve", lambda: nc.vector.tensor_scalar(out=hb[:, 0:NT * 64], in0=nownf, scalar1=-1.0, scalar2=None, op0=ALU.mult),
             reads=[b_nown], writes=[b_junk])
        c.op("dve", lambda: nc.vector.tensor_copy(out=hi32[:, 0:NT * 64], in_=hb[:, 0:NT * 64]), reads=[b_junk, b_T[0]], writes=[b_T[0]])
        c.op("dve", lambda: nc.vector.scalar_tensor_tensor(out=lo32[:, 0:NT * 64], in0=nownf, scalar=-1.0, in1=hi32[:, 0:NT * 64],
                                                           op0=ALU.mult, op1=ALU.subtract),
             reads=[b_nown, b_T[0], b_T[1]], writes=[b_T[1]])
        for x, src, sbuf_ in ((0, hi32, b_T[0]), (1, lo32, b_T[1])):
            for g8 in range(ncol // 128):
                bk = nbank()
                c.op("pe", lambda: nc.tensor.transpose(out=ps[bk][:, 0:128], in_=src[:, g8 * 128:(g8 + 1) * 128], identity=ident[:]),
                     reads=[sbuf_, b_ident], writes=[b_ps[bk]])
                ci = (x * (ncol // 128) + g8) % 2
                c.op("act", lambda: nc.scalar.copy(out=cTt[ci][:], in_=ps[bk][:, 0:128]), reads=[b_ps[bk]], writes=[b_cT[ci]])
                for b8 in range(8):
                    blk = g8 * 8 + b8
                    mm = blk // 4; s_ = blk % 4
                    if mm >= NT:
                        continue
                    c.op("pool", lambda: nc.gpsimd.dma_start(out=cqd[x, mm, :, s_ * 128:(s_ + 1) * 128],
                                                             in_=cTt[ci][b8 * H:(b8 + 1) * H, :]),
                         reads=[b_cT[ci]], writes=[b_cqd[mm]], dma="cT%d" % ci)

    def stage_attn(L, m):
        QT = arena
        qb = b_ar[0:16]
        import os as _os
        _skip = _os.environ.get("KSKIP", "").split(",")
        if "q" not in _skip:
            c.op("sp", lambda: nc.sync.dma_start(out=arena[0:64, 0:16, :],
                                                 in_=qbuf[m * D:(m + 1) * D, :].rearrange("(h d) t -> d h t", d=64)),
                 reads=[b_qbuf[m]], writes=qb, dma="qld")
        if "aug" not in _skip:
            c.op("sp", lambda: nc.sync.dma_start(out=arena[64:65, 0:16, :], in_=cqd[0, m:m + 1, :, :]),
                 reads=[b_cqd[m]], writes=qb, dma="qld1")
            c.op("sp", lambda: nc.sync.dma_start(out=arena[65:66, 0:16, :], in_=cqd[1, m:m + 1, :, :]),
                 reads=[b_cqd[m]], writes=qb, dma="qld2")
        if "ms" not in _skip:
            c.op("pool", lambda: nc.gpsimd.memset(junk[:], 0.0), writes=[b_junk])
            c.op("pool", lambda: nc.gpsimd.memset(arena[64:128, 16:32, :], 0.0), writes=b_ar[16:32])
        if stop == "B0":
            raise StopBuild()
        nt = 4 * m + 3
        steps = [("all", t) for t in range(nt)] + [("own", m)]
        bmv = biasm[:].rearrange("p a (t k) -> p a t k", k=4)
        ncv = negc[:].rearrange("p (t k) h -> p t k h", k=4)
        mdv = maddb[:, 0:NT * NB].rearrange("p (m t k) -> p m t k", m=NT, k=4)
        kvi = 0
        pti = 0
        for hg in range(4):
            for hh in range(4):
                h = hg * 4 + hh
                c.op("dve", lambda: nc.vector.tensor_tensor(out=bmv[:, hh, 0:nt, :], in0=ncv[:, 0:nt, :, h],
                                                            in1=mdv[:, m, 0:nt, :], op=ALU.add),
                     reads=[b_negc, b_maddb], writes=[b_biasm])
            if stop == "B1":
                raise StopBuild()
            ob = [4, 5, 6, 7]
            for si, (kind, t) in enumerate(steps):
                ki = kvi % 2; kvi += 1
                if kind == "all":
                    krow = (((t // 4) * 4 + hg) * 4 + (t % 4)) * 256
                    ksrc, kbuf, vbuf = kall, b_kall, b_vall
                    vview = vall.rearrange("(m k r p) e -> m r p k e", k=4, r=4, p=128)[t // 4, t % 4]
                else:
                    krow = m * D + hg * 256
                    ksrc, kbuf, vbuf = kloc, b_kloc, b_vloc
                    vview = vloc[m * TS:(m + 1) * TS, :].rearrange("(k p) e -> p k e", p=128)
                c.op("sp", lambda: nc.sync.dma_start(out=ktile[ki][0:64, :, :],
                                                     in_=ksrc[krow:krow + 256, :].rearrange("(h d) t -> d h t", d=64)),
                     reads=[kbuf], writes=[b_kt[ki]], dma="kt%d" % ki)
                c.op("sp", lambda: nc.sync.dma_start(out=vtile[ki][:], in_=vview[:, :, hg * 260:(hg + 1) * 260]),
                     reads=[vbuf], writes=[b_vt[ki]], dma="vt%d" % ki)
                if stop == "B2":
                    raise StopBuild()
                for hh in range(4):
                    h = hg * 4 + hh
                    for kb in range(4):
                        if stop == "B3" and kb == 1:
                            raise StopBuild()
                        sbk = nbank(0, 4)
                        pi = pti % NPT; pti += 1
                        q0 = kb * 128 if kind == "own" else 0
                        if kind == "all":
                            c.op("pe", lambda: nc.tensor.matmul(ps[sbk][:, :], lhsT=ktile[ki][0:66, hh, kb * 128:(kb + 1) * 128],
                                                                rhs=arena[0:66, h, :], start=True, stop=True),
                                 reads=[b_kt[ki], b_ar[h]], writes=[b_ps[sbk]])
                            bias_ap = biasm[:, hh, t * 4 + kb:t * 4 + kb + 1]
                            brd = b_biasm
                        else:
                            c.op("pe", lambda: nc.tensor.matmul(ps[sbk][:, q0:TS], lhsT=ktile[ki][0:66, hh, kb * 128:(kb + 1) * 128],
                                                                rhs=arena[0:66, h, q0:TS], start=True, stop=False),
                                 reads=[b_kt[ki], b_ar[h]], writes=[b_ps[sbk]])
                            c.op("pe", lambda: nc.tensor.matmul(ps[sbk][:, q0:q0 + 128], lhsT=identb[:], rhs=trim[:],
                                                                start=False, stop=True),
                                 reads=[b_identb, b_trim], writes=[b_ps[sbk]])
                            bias_ap = nown[:, m, kb * H + h:kb * H + h + 1]
                            brd = b_nown
                        c.op("act", lambda: nc.scalar.activation(out=ptile[pi][:, q0:TS], in_=ps[sbk][:, q0:TS], func=AF.Exp,
                                                                 bias=bias_ap, scale=1.0),
                             reads=[b_ps[sbk], brd], writes=[b_pt[pi]])
                        first = (si == 0 and kb == 0)
                        last = (kind == "own" and kb == 3)
                        c.op("pe", lambda: nc.tensor.matmul(ps[ob[hh]][0:65, q0:TS], lhsT=vtile[ki][:, kb, hh * 65:(hh + 1) * 65],
                                                            rhs=ptile[pi][:, q0:TS], start=first, stop=last),
                             reads=[b_vt[ki], b_pt[pi]], writes=[b_ps[ob[hh]]])
            if stop == "B4":
                raise StopBuild()
            for hh in range(4):
                h = hg * 4 + hh
                c.op("act", lambda: nc.scalar.copy(out=OTs[0:65, :], in_=ps[ob[hh]][0:65, :]),
                     reads=[b_ps[ob[hh]]], writes=[b_OTs])
                c.op("dve", lambda: nc.vector.reciprocal(out=rinv[64:65, :], in_=OTs[64:65, :]),
                     reads=[b_OTs], writes=[b_rinv])
                c.op("dve", lambda: nc.vector.tensor_copy(out=junk[64:65, 0:TS], in_=rinv[64:65, :]),
                     reads=[b_rinv], writes=[b_junk])
                c.op("dve", lambda: nc.vector.tensor_tensor(out=rinv[64:65, :], in0=rinv[64:65, :], in1=junk[64:65, 0:TS],
                                                            op=ALU.subtract), reads=[b_rinv, b_junk], writes=[b_rinv])
                c.op("dve", lambda: nc.vector.tensor_copy(out=junk[64:65, TS:2 * TS], in_=rinv[64:65, :]),
                     reads=[b_rinv], writes=[b_junk])
                bk = nbank(0, 4)
                c.op("pe", lambda: nc.tensor.matmul(ps[bk][:, :], lhsT=onesb[:, :], rhs=junk[:, 0:TS],
                                                    start=True, stop=False),
                     reads=[b_onesb, b_junk], writes=[b_ps[bk]])
                c.op("pe", lambda: nc.tensor.matmul(ps[bk][:, :], lhsT=onesb[:, :], rhs=junk[:, TS:2 * TS],
                                                    start=False, stop=True),
                     reads=[b_onesb, b_junk], writes=[b_ps[bk]])
                c.op("dve", lambda: nc.vector.tensor_tensor(out=arena[0:64, 16 + h, :], in0=OTs[0:64, :], in1=ps[bk][0:64, :],
                                                            op=ALU.mult),
                     reads=[b_OTs, b_ps[bk]], writes=[b_ar[16 + h]])

    def conv_halo_prep(L):
        key = "cin%d" % L
        n2 = 2 * NT
        c.op("pool", lambda: nc.gpsimd.memset(Tt[0][:], 0.0), writes=[b_T[0]])
        c.op("pool", lambda: nc.gpsimd.memset(Tt[1][:], 0.0), writes=[b_T[1]])
        c.op("sp", lambda: nc.sync.dma_start(out=Tt[0][0:8 * NT, :], in_=xhall[:, :]), reads=[b_xhall], writes=[b_T[0]], dma="xhl")
        for h2 in range(2):
            bk = nbank()
            terms = split3(Tt[0][:, h2 * TS:(h2 + 1) * TS], [b_T[0]], TS)
            for ti_, (hh_, hb_) in enumerate(terms):
                c.op("pe", lambda: nc.tensor.matmul(ps[bk][:, :], lhsT=selhb[:, :], rhs=hh_,
                                                    start=(ti_ == 0), stop=(ti_ == 2)),
                     reads=[b_selhb, hb_], writes=[b_ps[bk]])
            c.op("dve", lambda: nc.vector.tensor_copy(out=Tt[1][0:n2, h2 * TS:(h2 + 1) * TS], in_=ps[bk][0:n2, :]),
                 reads=[b_ps[bk]], writes=[b_T[1]])
        c.op("act", lambda: nc.scalar.activation(out=junk[0:n2, :], in_=Tt[1][0:n2, :], func=AF.Square,
                                                 accum_out=ss[0:n2, 7:8]), reads=[b_T[1]], writes=[b_junk, b_ss])
        c.op("act", lambda: nc.scalar.activation(out=rstd[0:n2, 7:8], in_=ss[0:n2, 7:8], func=AF.Sqrt,
                                                 scale=1.0 / D, bias=EPS), reads=[b_ss], writes=[b_rstd])
        c.op("dve", lambda: nc.vector.reciprocal(out=rstd[0:n2, 7:8], in_=rstd[0:n2, 7:8]), reads=[b_rstd], writes=[b_rstd])
        c.op("act", lambda: nc.scalar.activation(out=Tt[1][0:n2, :], in_=Tt[1][0:n2, :], func=AF.Copy,
                                                 scale=rstd[0:n2, 7:8]), reads=[b_T[1], b_rstd], writes=[b_T[1]])
        for half in range(2):
            bk = nbank()
            for kk in range(4):
                k = half * 4 + kk
                c.op("pe", lambda: nc.tensor.transpose(out=ps[bk][:, kk * 128:(kk + 1) * 128], in_=Tt[1][:, k * 128:(k + 1) * 128],
                                                       identity=ident[:]),
                     reads=[b_T[1], b_ident], writes=[b_ps[bk]])
            for kk in range(4):
                k = half * 4 + kk
                c.op("dve", lambda: nc.vector.tensor_scalar(out=hnTh[:, k, :], in0=ps[bk][:, kk * 128:kk * 128 + n2],
                                                            scalar1=gTT[L % 2][:, k:k + 1], scalar2=None, op0=ALU.mult),
                     reads=[b_ps[bk], b_gTT[L % 2]], writes=[b_hnTh])
        for q in range(2):
            WC, bWC = wblock(key, 0, D, D + q * 512, 512)
            WU, bWU = wblock(key, 0, D, 2 * D + q * 512, 512)
            for fq in range(4):
                fc = q * 4 + fq
                bk = nbank()
                for k in range(8):
                    c.op("pe", lambda: nc.tensor.matmul(ps[bk][:, 0:n2], lhsT=WC[:, k, fq * 128:(fq + 1) * 128],
                                                        rhs=hnTh[:, k, :], start=(k == 0), stop=(k == 7)),
                         reads=[bWC, b_hnTh], writes=[b_ps[bk]])
                c.op("act", lambda: nc.scalar.copy(out=chh[:], in_=ps[bk][:, 0:n2]), reads=[b_ps[bk]], writes=[b_chh])
                bk = nbank()
                for k in range(8):
                    c.op("pe", lambda: nc.tensor.matmul(ps[bk][:, 0:n2], lhsT=WU[:, k, fq * 128:(fq + 1) * 128],
                                                        rhs=hnTh[:, k, :], start=(k == 0), stop=(k == 7)),
                         reads=[bWU, b_hnTh], writes=[b_ps[bk]])
                c.op("dve", lambda: nc.vector.tensor_tensor(out=zh[:, fc, :], in0=ps[bk][:, 0:n2], in1=chh[:], op=ALU.mult),
                     reads=[b_ps[bk], b_chh], writes=[b_zh])

    def stage_conv(L, m):
        key = "cin%d" % L
        norm_to_T(0, L)
        for q in range(2):
            WC, bWC = wblock(key, 0, D, D + q * 512, 512)
            WU, bWU = wblock(key, 0, D, 2 * D + q * 512, 512)
            WB, bWB = wblock(key, 0, D, q * 512, 512)
            for fq in range(4):
                fc = q * 4 + fq
                bk = nbank()
                for k in range(8):
                    c.op("pe", lambda: nc.tensor.matmul(ps[bk][:, :], lhsT=WC[:, k, fq * 128:(fq + 1) * 128],
                                                        rhs=hnT[:, k, :], start=(k == 0), stop=(k == 7)),
                         reads=[bWC, b_hnT], writes=[b_ps[bk]])
                c.op("act", lambda: nc.scalar.copy(out=csb[:], in_=ps[bk][:, :]), reads=[b_ps[bk]], writes=[b_csb])
                bk = nbank()
                for k in range(8):
                    c.op("pe", lambda: nc.tensor.matmul(ps[bk][:, :], lhsT=WU[:, k, fq * 128:(fq + 1) * 128],
                                                        rhs=hnT[:, k, :], start=(k == 0), stop=(k == 7)),
                         reads=[bWU, b_hnT], writes=[b_ps[bk]])
                c.op("dve", lambda: nc.vector.tensor_tensor(out=zt[:, 2:TS + 2], in0=ps[bk][:, :], in1=csb[:], op=ALU.mult),
                     reads=[b_ps[bk], b_csb], writes=[b_zt])
                c.op("pool", lambda: nc.gpsimd.tensor_copy(out=zt[:, 0:2], in_=zh[:, fc, 2 * m:2 * m + 2]),
                     reads=[b_zh], writes=[b_zt])
                c.op("dve", lambda: nc.vector.tensor_scalar(out=zc[:], in0=zt[:, 0:TS], scalar1=gTT[1][:, 48 + fc:48 + fc + 1],
                                                             scalar2=None, op0=ALU.mult),
                     reads=[b_zt, b_gTT[1]], writes=[b_zc])
                for tap in (1, 2):
                    c.op("dve", lambda: nc.vector.scalar_tensor_tensor(out=zc[:], in0=zt[:, tap:TS + tap],
                                                                        scalar=gTT[1][:, 48 + tap * 8 + fc:48 + tap * 8 + fc + 1], in1=zc[:],
                                                                        op0=ALU.mult, op1=ALU.add),
                         reads=[b_zt, b_gTT[1], b_zc], writes=[b_zc])
                bk = nbank()
                for k in range(8):
                    c.op("pe", lambda: nc.tensor.matmul(ps[bk][:, :], lhsT=WB[:, k, fq * 128:(fq + 1) * 128],
                                                        rhs=hnT[:, k, :], start=(k == 0), stop=(k == 7)),
                         reads=[bWB, b_hnT], writes=[b_ps[bk]])
                c.op("dve", lambda: nc.vector.tensor_tensor(out=arena[:, 16 + fc, :], in0=ps[bk][:, :], in1=zc[:], op=ALU.mult),
                     reads=[b_ps[bk], b_zc], writes=[b_ar[16 + fc]])

    def stage_post(L, m):
        attn = (L % 2 == 0)
        okey = ("aout%d" if attn else "cout%d") % L
        g1, bg1 = load_gbc(L, 1)
        Ws = []
        for c2 in range(2):
            if attn:
                Ws.append([wblock(okey, hb * 512, 512, c2 * 512, 512, kp=64) for hb in range(2)])
            else:
                Ws.append([wblock(okey, 0, D, c2 * 512, 512)])
        for s in range(4):
            bks = []
            for c2 in range(2):
                bk = nbank(); bks.append(bk)
                if attn:
                    for h in range(16):
                        W, bW = Ws[c2][h // 8]
                        c.op("pe", lambda: nc.tensor.matmul(ps[bk][:, :], lhsT=arena[:, 16 + h, s * 128:(s + 1) * 128],
                                                            rhs=W[:, h % 8, :], start=(h == 0), stop=(h == 15)),
                             reads=[bW, b_ar[16 + h]], writes=[b_ps[bk]])
                else:
                    W, bW = Ws[c2][0]
                    for k in range(8):
                        c.op("pe", lambda: nc.tensor.matmul(ps[bk][:, :], lhsT=arena[:, 16 + k, s * 128:(s + 1) * 128],
                                                            rhs=W[:, k, :], start=(k == 0), stop=(k == 7)),
                             reads=[bW, b_ar[16 + k]], writes=[b_ps[bk]])
            res_norm(s, [ps[bks[0]][:, :], ps[bks[1]][:, :]], [b_ps[bks[0]], b_ps[bks[1]]], g1, bg1)
        if stop == "C1":
            raise StopBuild()
        g3, bg3 = load_gbc(L, 3)
        norm_to_T(1, L)
        ukey = "up%d" % L
        for ub in range(8):
            W, bW = wblock(ukey, 0, D, ub * 512, 512)
            for fq in range(4):
                ff = ub * 4 + fq
                bk = nbank()
                for k in range(8):
                    c.op("pe", lambda: nc.tensor.matmul(ps[bk][:, :], lhsT=W[:, k, fq * 128:(fq + 1) * 128],
                                                        rhs=hnT[:, k, :], start=(k == 0), stop=(k == 7)),
                         reads=[bW, b_hnT], writes=[b_ps[bk]])
                ti = ff % 2
                c.op("act", lambda: nc.scalar.activation(out=Tt[ti][:, 0:TS], in_=ps[bk][:, :], func=AF.Relu),
                     reads=[b_ps[bk]], writes=[b_T[ti]])
                c.op("pool", lambda: nc.gpsimd.tensor_tensor(out=arena[:, ff, :], in0=Tt[ti][:, 0:TS], in1=Tt[ti][:, 0:TS],
                                                             op=ALU.mult),
                     reads=[b_T[ti]], writes=[b_ar[ff]])
        dkey = "dn%d" % L
        for pc in range(8):
            W, bW = wblock(dkey, pc * 512, 512, 0, D)
            for s in range(4):
                for c2 in range(2):
                    bk = s * 2 + c2
                    for fq in range(4):
                        ff = pc * 4 + fq
                        c.op("pe", lambda: nc.tensor.matmul(ps[bk][:, :], lhsT=arena[:, ff, s * 128:(s + 1) * 128],
                                                            rhs=W[:, fq, c2 * 512:(c2 + 1) * 512],
                                                            start=(ff == 0), stop=(ff == 31)),
                             reads=[bW, b_ar[ff]], writes=[b_ps[bk]])
        for s in range(4):
            res_norm(s, [ps[2 * s][:, :], ps[2 * s + 1][:, :]], [b_ps[2 * s], b_ps[2 * s + 1]], g3, bg3)
        st["bank"] = 0
        if stop == "C2":
            raise StopBuild()
        g5, bg5 = load_gbc(L, 5)
        norm_to_T(2, L)
        c.op("sp", lambda: nc.sync.dma_start(out=ptl[:], in_=p_in[m * TS:(m + 1) * TS, :].rearrange("(s p) e -> p s e", p=128)),
             writes=[b_ptl], dma="ptl")
        for kc in range(2):
            bk = nbank()
            for s in range(4):
                c.op("pe", lambda: nc.tensor.transpose(out=ps[bk][:, s * 128:(s + 1) * 128], in_=ptl[:, s, kc * 128:(kc + 1) * 128],
                                                       identity=ident[:]),
                     reads=[b_ptl, b_ident], writes=[b_ps[bk]])
            c.op("act", lambda: nc.scalar.copy(out=pT[:, kc, :], in_=ps[bk][:, :]), reads=[b_ps[bk]], writes=[b_pT])
        gkey = "pg%d" % L; pkey = "pp%d" % L
        WG = [wblock(gkey, 0, D, c2 * 512, 512) for c2 in range(2)]
        WP, bWP = wblock(pkey, 0, PLE, 0, D)
        for s in range(4):
            for c2 in range(2):
                bg_ = nbank()
                W, bW = WG[c2]
                for k in range(8):
                    c.op("pe", lambda: nc.tensor.matmul(ps[bg_][:, :], lhsT=hnT[:, k, s * 128:(s + 1) * 128], rhs=W[:, k, :],
                                                        start=(k == 0), stop=(k == 7)),
                         reads=[bW, b_hnT], writes=[b_ps[bg_]])
                be_ = nbank()
                for kc in range(2):
                    c.op("pe", lambda: nc.tensor.matmul(ps[be_][:, :], lhsT=pT[:, kc, s * 128:(s + 1) * 128],
                                                        rhs=WP[:, kc, c2 * 512:(c2 + 1) * 512], start=(kc == 0), stop=(kc == 1)),
                         reads=[bWP, b_pT], writes=[b_ps[be_]])
                c.op("act", lambda: nc.scalar.activation(out=Tt[0][:, c2 * TS:(c2 + 1) * TS], in_=ps[bg_][:, :], func=AF.Sigmoid),
                     reads=[b_ps[bg_]], writes=[b_T[0]])
                c.op("dve", lambda: nc.vector.tensor_tensor(out=Tt[1][:, c2 * TS:(c2 + 1) * TS], in0=ps[be_][:, :],
                                                            in1=Tt[0][:, c2 * TS:(c2 + 1) * TS], op=ALU.mult),
                     reads=[b_ps[be_], b_T[0]], writes=[b_T[1]])
            res_norm(s, [Tt[1][:, 0:TS], Tt[1][:, TS:2 * TS]], [b_T[1], b_T[1]], g5, bg5)

    def finish():
        for en in ("sp", "pool", "act", "dve", "pe"):
            for k in c.sems:
                if c.cnt[k] > 0:
                    c.engs[en].wait_ge(c.sems[k], c.cnt[k])
        es.close()
        return nc, c

    cast_weights()
    b_xin = []
    try:
        if phase == "pre":
            load_layer_consts(0)
            for m in range(NT):
                load_x(m, x_in, b_xin)
                stage_attn_pre(0, m)
        elif phase == "attn":
            load_layer_consts(LA)
            attn_exchange(LA)
            for m in range(NT):
                load_x(m, x_in, b_xin)
                stage_attn(LA, m)
                stage_post(LA, m)
                store_x(m, out_d, [], halo=True)
        else:
            load_layer_consts(LC)
            if phase == "convpre":
                load_layer_consts(2)
            conv_halo_prep(LC)
            for m in range(NT):
                load_x(m, x_in, b_xin)
                stage_conv(LC, m)
                stage_post(LC, m)
                if phase == "convpre":
                    stage_attn_pre(2, m)
                store_x(m, out_d, [], halo=False)
    except StopBuild:
        pass
    return finish()


def host_consts(NT, j):
    NB = 16 * NT
    madd = np.zeros((NT, NB), np.float32)
    for m in range(NT):
        for kbi in range(NB):
            t = kbi // 4
            if not (t < 4 * m + j):
                madd[m, kbi] = NEG
    sel = np.zeros((128, 4), np.float32); sel[:, j] = 1.0
    maddb = np.broadcast_to(madd.reshape(1, NT * NB), (128, NT * NB)).astype(np.float32).copy()
    sel = np.concatenate([sel, np.zeros((128, 12), np.float32)], axis=1)
    selh = np.zeros((128, 128), np.float32)
    for m in range(NT):
        g = 4 * m + j
        if g >= 1:
            r2 = (g - 1) % 4; m2 = (g - 1) // 4
            for i in range(2):
                selh[r2 * 2 * NT + m2 * 2 + i, m * 2 + i] = 1.0
    return maddb, sel, selh


def _shard_idx(NT, j):
    return np.concatenate([np.arange((4 * m + j) * TS, (4 * m + j + 1) * TS) for m in range(NT)])


_CACHE = {}


def _run(phase, NT, in_maps):
    key = (phase, NT)
    if key not in _CACHE:
        _CACHE[key] = build(NT, phase)[0]
    res = run_bass_kernel_spmd(_CACHE[key], in_maps, core_ids=list(range(8)))
    return res.results


def _gather_kv(res, NT):
    out = []
    for b in range(2):
        rs = [res[4 * b + r] for r in range(4)]
        kall = np.concatenate([np.asarray(rs[r]["kloc"])[ci * 256:(ci + 1) * 256] for ci in range(NT * 4) for r in range(4)], axis=0)
        vall = np.concatenate([np.asarray(rs[r]["vloc"])[ci * 128:(ci + 1) * 128] for ci in range(NT * 4) for r in range(4)], axis=0)
        lfall = np.concatenate([np.asarray(rs[r]["lfloc"]) for r in range(4)], axis=0)
        out.append((kall, vall, lfall))
    return out


def kernel(**inputs):
    f = lambda k: np.asarray(inputs[k], np.float32)
    x = f("x"); p = f("p"); ng = f("norm_g"); bfg = f("b_forget"); cw = f("conv_w")
    wai = f("w_attn_in"); wao = f("w_attn_out"); wci = f("w_conv_in"); wco = f("w_conv_out")
    wup = f("w_mlp_up"); wdn = f("w_mlp_down"); wpp = f("w_ple_proj"); wpg = f("w_ple_gate")
    B, S, _ = x.shape
    NT = S // (4 * TS)
    cores = list(range(8))
    idx = [_shard_idx(NT, c % 4) for c in cores]
    consts = []
    for c in cores:
        maddb, sel, selh = host_consts(NT, c % 4)
        consts.append({"maddb": maddb, "sel": sel, "selh": selh})
    xs = [np.ascontiguousarray(x[c // 4][idx[c]]) for c in cores]
    psh = [[np.ascontiguousarray(p[L, c // 4][idx[c]]) for c in cores] for L in range(4)]

    def mlp_w(L):
        return {"w_up": wup[L], "w_dn": wdn[L], "w_pp": wpp[L], "w_pg": wpg[L]}

    r = _run("pre", NT, [dict(consts[c], x=xs[c], norm_g_pre=ng[0], b_forget=bfg[0], w_ain=wai[0]) for c in cores])
    for L in (0, 2):
        a = L // 2
        kv = _gather_kv(r, NT)
        im = []
        for c in cores:
            kall, vall, lfall = kv[c // 4]
            d = dict(consts[c], x=xs[c], p=psh[L][c], norm_g=ng[L], w_out=wao[a], qbuf=np.asarray(r[c]["qbuf"]),
                     kloc=np.asarray(r[c]["kloc"]), vloc=np.asarray(r[c]["vloc"]), kall=kall, vall=vall, lfall=lfall)
            d.update(mlp_w(L))
            im.append(d)
        r2 = _run("attn", NT, im)
        xs = [np.asarray(r2[c]["out"]) for c in cores]
        xh = [np.concatenate([np.asarray(r2[4 * b + rr]["xhloc"]) for rr in range(4)], axis=0) for b in range(2)]
        Lc = L + 1
        im = []
        for c in cores:
            d = dict(consts[c], x=xs[c], p=psh[Lc][c], norm_g=ng[Lc], conv_w=cw[a], w_cin=wci[a], w_out=wco[a],
                     xhall=xh[c // 4])
            d.update(mlp_w(Lc))
            if L == 0:
                d.update(norm_g_pre=ng[2], b_forget=bfg[1], w_ain=wai[1])
            im.append(d)
        r = _run("convpre" if L == 0 else "conv", NT, im)
        xs = [np.asarray(r[c]["out"]) for c in cores]
    out = np.zeros((B, S, D), np.float32)
    for c in cores:
        out[c // 4][idx[c]] = xs[c]
    return out
```

```python
import numpy as np
from contextlib import ExitStack
import concourse.bass as bass
import concourse.mybir as mybir
from concourse.bass_utils import run_bass_kernel_spmd

F32 = mybir.dt.float32
BF16 = mybir.dt.bfloat16
AF = mybir.ActivationFunctionType
ALU = mybir.AluOpType

D = 1024
H = 16
DH = 64
DFF = 4096
PLE = 256
TS = 512
EPS = 1e-6
NEG = -30000.0
GRP4 = [[0, 1, 2, 3], [4, 5, 6, 7]]
GRP8 = [[0, 1, 2, 3, 4, 5, 6, 7]]


class StopBuild(Exception):
    pass


class Buf:
    __slots__ = ("name", "w", "r")

    def __init__(self, name):
        self.name = name
        self.w = None
        self.r = {}


class Ctx:
    def __init__(self, nc, es):
        self.nc = nc
        self.es = es
        self.engs = {"pe": nc.tensor, "dve": nc.vector, "act": nc.scalar,
                     "pool": nc.gpsimd, "sp": nc.sync}
        self.sems = {}
        self.cnt = {}
        self.waited = {}
        for k in self.engs:
            self._mksem("E_" + k)
        self.ninstr = 0

    def _mksem(self, key):
        self.sems[key] = self.es.enter_context(self.nc.semaphore(key))
        self.cnt[key] = 0

    def _wait(self, engname, ev):
        if ev is None:
            return
        key, val = ev
        wk = (engname, key)
        if self.waited.get(wk, 0) >= val:
            return
        self.engs[engname].wait_ge(self.sems[key], val)
        self.waited[wk] = val

    def op(self, engname, fn, reads=(), writes=(), dma=None, inc=16):
        own = "E_" + engname
        deps = []
        for b in reads:
            if b.w is not None:
                deps.append(b.w)
        for b in writes:
            if b.w is not None:
                deps.append(b.w)
            deps.extend(b.r.items())
        for ev in deps:
            if dma is None and ev[0] == own and engname == "pe":
                continue
            self._wait(engname, ev)
        ins = fn()
        if dma is None:
            self.cnt[own] += 1
            ev = (own, self.cnt[own])
            ins.then_inc(self.sems[own], 1)
        else:
            if dma not in self.sems:
                self._mksem(dma)
            self.cnt[dma] += inc
            ev = (dma, self.cnt[dma])
            ins.then_inc(self.sems[dma], inc)
        for b in reads:
            if b.r.get(ev[0], 0) < ev[1]:
                b.r[ev[0]] = ev[1]
        for b in writes:
            b.w = ev
            b.r = {}
        self.ninstr += 1
        return ev


def build(NT, phase, stop=None):
    SL = NT * TS
    NB = 16 * NT
    NGT = 4 * NT
    nc = bass.Bass("TRN2", target_bir_lowering=False)
    dt = nc.dram_tensor

    def ext(name, shape, dtype=F32):
        return dt(name, list(shape), dtype, kind="ExternalInput").ap()

    def exto(name, shape, dtype=F32):
        return dt(name, list(shape), dtype, kind="ExternalOutput").ap()

    has_pre = phase in ("pre", "convpre")
    has_attn = phase == "attn"
    has_conv = phase in ("convpre", "conv")
    LA = 0
    LC = 1
    x_in = ext("x", [SL, D])
    ng_by_L = {}
    wsrc = {}
    if has_attn or has_conv:
        p_in = ext("p", [SL, PLE])
        Lm = LA if has_attn else LC
        ng_by_L[Lm] = ext("norm_g", [6, D])
        wsrc["aout%d" % Lm if has_attn else "cout%d" % Lm] = ext("w_out", [D, D])
        wsrc["up%d" % Lm] = ext("w_up", [D, DFF])
        wsrc["dn%d" % Lm] = ext("w_dn", [DFF, D])
        wsrc["pp%d" % Lm] = ext("w_pp", [PLE, D])
        wsrc["pg%d" % Lm] = ext("w_pg", [D, D])
        out_d = exto("out", [SL, D])
    if has_conv:
        cw_in = ext("conv_w", [3, D])
        wsrc["cin%d" % LC] = ext("w_cin", [D, 3 * D])
        xhall = ext("xhall", [8 * NT, D])
    if has_pre:
        LP = 0 if phase == "pre" else 2
        ng_by_L[LP] = ext("norm_g_pre", [6, D])
        bf_in = ext("b_forget", [H])
        wsrc["ain%d" % LP] = ext("w_ain", [D, 3 * D + H])
        qbuf = exto("qbuf", [NT * D, TS], BF16)
        kloc = exto("kloc", [NT * D, TS], BF16)
        vloc = exto("vloc", [NT * TS, H * 65], BF16)
        lfloc = exto("lfloc", [128, NT * 64])
    if has_attn:
        qbuf = ext("qbuf", [NT * D, TS], BF16)
        kloc = ext("kloc", [NT * D, TS], BF16)
        vloc = ext("vloc", [NT * TS, H * 65], BF16)
        kall = ext("kall", [4 * NT * D, TS], BF16)
        vall = ext("vall", [4 * NT * TS, H * 65], BF16)
        lfall = ext("lfall", [512, NT * 64])
        xhloc = exto("xhloc", [2 * NT, D])
        cqd = dt("cqd", [2, NT, H, TS], BF16).ap()
    maddb_in = ext("maddb", [128, NT * NB])
    sel_in = ext("sel", [128, 16])
    selh_in = ext("selh", [128, 128])
    b_qbuf = [Buf("qbuf%d" % m) for m in range(NT)]
    b_cqd = [Buf("cqd%d" % m) for m in range(NT)]
    b_kloc, b_kall, b_vloc, b_vall = Buf("kloc"), Buf("kall"), Buf("vloc"), Buf("vall")
    b_lfloc, b_lfall, b_xhloc, b_xhall = Buf("lfloc"), Buf("lfall"), Buf("xhloc"), Buf("xhall")

    es = ExitStack()
    c = Ctx(nc, es)

    def sb(name, shape, dtype):
        t = es.enter_context(nc.sbuf_tensor(name, list(shape), dtype))
        import os as _o
        if _o.environ.get("KPRINT"):
            print("SBUF", name, shape, t[:].tensor)
        return t

    def fence(buf):
        c.op("pool", lambda: nc.gpsimd.memset(fdum[:], 0.0), reads=[buf], writes=[buf, b_fdum])

    fdum = sb("fdum", [128, 8], F32); b_fdum = Buf("fdum")
    xt = sb("xt", [128, 4, D], F32); b_xt = [Buf("xt%d" % s) for s in range(4)]
    Tt = [sb("T%d" % i, [128, D], F32) for i in range(3)]; b_T = [Buf("T%d" % i) for i in range(3)]
    junk = sb("junk", [128, D], BF16); b_junk = Buf("junk")
    hnT = sb("hnT", [128, 8, TS], BF16); b_hnT = Buf("hnT")
    NSLOT = 4
    wslot = [sb("ws%d" % i, [128, 4096], BF16) for i in range(NSLOT)]
    b_ws = [Buf("ws%d" % i) for i in range(NSLOT)]
    arena = sb("arena", [128, 32, TS], BF16); b_ar = [Buf("ar%d" % i) for i in range(32)]
    ktile = [sb("kt%d" % i, [66, 4, TS], BF16) for i in range(2)]; b_kt = [Buf("kt%d" % i) for i in range(2)]
    vtile = [sb("vt%d" % i, [128, 4, 4 * 65], BF16) for i in range(2)]; b_vt = [Buf("vt%d" % i) for i in range(2)]
    NPT = 3
    ptile = [sb("pt%d" % i, [128, TS], BF16) for i in range(NPT)]; b_pt = [Buf("pt%d" % i) for i in range(NPT)]
    negc = sb("negc", [128, NB, H], F32); b_negc = Buf("negc")
    nown = sb("nown", [128, NT, 4 * H], F32); b_nown = Buf("nown")
    biasm = sb("biasm", [128, 4, NB], F32); b_biasm = Buf("biasm")
    maddb = sb("maddb_s", [128, NT * NB], F32)
    selt = sb("sel_s", [128, 16], F32); b_sel = Buf("sel"); b_maddb = Buf("maddb")
    selh = sb("selh_s", [128, 128], F32); b_selh = Buf("selh")
    trim = sb("trim", [128, 128], BF16); b_trim = Buf("trim")
    trimf = sb("trimf", [128, 128], F32); b_trimf = Buf("trimf")
    ident = sb("ident", [128, 128], F32); b_ident = Buf("ident")
    identb = sb("identb", [128, 128], BF16); b_identb = Buf("identb")
    ones_f = sb("ones_f", [128, 128], F32); b_ones = Buf("ones")
    OTs = Tt[2][0:65, 0:TS]; b_OTs = b_T[2]
    rinv = Tt[2][:, TS:2 * TS]; b_rinv = b_T[2]
    tric = sb("tric", [128, 128], F32); b_tric = Buf("tric")
    tricb = sb("tricb", [128, 128], BF16); b_tricb = Buf("tricb")
    onesb = sb("onesb", [128, 128], BF16); b_onesb = Buf("onesb")
    selhb = sb("selhb", [128, 128], BF16); b_selhb = Buf("selhb")
    cTt = [sb("cT%d" % i, [128, 128], BF16) for i in range(2)]; b_cT = [Buf("cT%d" % i) for i in range(2)]
    import os as _os0
    if _os0.environ.get("KDBG") == "lfcown":
        _lfc = sb("lfcx", [16, 2 * TS], F32); b_lfcx = Buf("lfcx")
        lfc = [_lfc[:, i * TS:(i + 1) * TS] for i in range(2)]; b_lfc = [b_lfcx, b_lfcx]
    elif _os0.environ.get("KDBG") == "lfcT2":
        lfc = [Tt[2][0:16, i * TS:(i + 1) * TS] for i in range(2)]; b_lfc = [b_T[2], b_T[2]]
    elif _os0.environ.get("KDBG") == "lfcxt":
        lfc = [xt[0:16, 0, i * TS:(i + 1) * TS] for i in range(2)]; b_lfc = [b_xt[0], b_xt[0]]
    else:
        lfc = [Tt[0][0:16, i * TS:(i + 1) * TS] for i in range(2)]; b_lfc = [b_T[0], b_T[0]]
    cc = [Tt[1][0:16, i * TS:(i + 1) * TS] for i in range(2)]; b_cc = [b_T[1], b_T[1]]
    ccp = [Tt[1][:, i * TS:(i + 1) * TS] for i in range(2)]
    cqacc = Tt[2][0:16, 0:TS]; b_cqacc = b_T[2]
    vst = sb("vst", [128, 4, H, 65], BF16); b_vst = Buf("vst")
    qkst = [sb("qkst%d" % i, [128, 2, TS], BF16) for i in range(2)]; b_qkst = [Buf("qkst%d" % i) for i in range(2)]
    csb = Tt[0][:, 0:TS]; b_csb = b_T[0]
    zc = Tt[0][:, TS:2 * TS]; b_zc = b_T[0]
    zt = Tt[1][:, 0:TS + 2]; b_zt = b_T[1]
    zh = sb("zh", [128, 8, 2 * NT], F32); b_zh = Buf("zh")
    hnTh = sb("hnTh", [128, 8, 2 * NT], BF16); b_hnTh = Buf("hnTh")
    chh = sb("chh", [128, 2 * NT], F32); b_chh = Buf("chh")
    gbc = [sb("gbc%d" % i, [128, D], F32) for i in range(2)]; b_gbc = [Buf("gbc%d" % i) for i in range(2)]
    gsrc = sb("gsrc", [128, 128], F32); b_gsrc = Buf("gsrc")
    gTT = [sb("gTT%d" % i, [128, 128], F32) for i in range(2)]; b_gTT = [Buf("gTT%d" % i) for i in range(2)]
    bfb = sb("bfb", [128, 4, H], F32); b_bfb = Buf("bfb")
    ptl = sb("ptl", [128, 4, PLE], F32); b_ptl = Buf("ptl")
    pT = sb("pT", [128, 2, TS], BF16); b_pT = Buf("pT")
    ss = sb("ss", [128, 8], F32); b_ss = Buf("ss")
    rstd = sb("rstd", [128, 8], F32); b_rstd = Buf("rstd")
    fz = sb("fz", [128, 4, H], F32); b_fz = Buf("fz")
    fa = sb("fa", [128, 4, H], F32); b_fa = Buf("fa")
    fl = sb("fl", [128, 4, H], F32); b_fl = Buf("fl")
    lfst = Tt[2][0:16, 0:TS]; b_lfst = b_T[2]

    ps = [es.enter_context(nc.psum_tensor("ps%d" % i, [128, TS], F32)) for i in range(8)]
    b_ps = [Buf("ps%d" % i) for i in range(8)]
    st = {"bank": 0, "ws": 0, "g": 0}

    def nbank(lo=0, hi=8):
        b = st["bank"]
        if b < lo or b >= hi:
            b = lo
        st["bank"] = b + 1 if b + 1 < hi else lo
        return b

    def wload(dram_view, shape):
        i = st["ws"]; st["ws"] = (i + 1) % NSLOT
        n = int(np.prod(shape[1:]))
        v = wslot[i][0:shape[0], 0:n]
        if len(shape) == 3:
            v = v.rearrange("p (a b) -> p a b", a=shape[1])
        c.op("sp", lambda: nc.sync.dma_start(out=v, in_=dram_view), reads=[], writes=[b_ws[i]],
             dma="ws%d" % i)
        return v, b_ws[i]

    T2b = Tt[2][:].bitcast(BF16)

    def split3(src, src_bufs, n):
        h1 = junk[:, 0:n]; h2 = junk[:, TS:TS + n]; h3 = T2b[:, 0:n]
        r = Tt[2][:, TS:TS + n]
        c.op("dve", lambda: nc.vector.tensor_copy(out=h1, in_=src), reads=src_bufs, writes=[b_junk])
        c.op("dve", lambda: nc.vector.tensor_tensor(out=r, in0=src, in1=h1, op=ALU.subtract),
             reads=src_bufs + [b_junk], writes=[b_T[2]])
        c.op("dve", lambda: nc.vector.tensor_copy(out=h2, in_=r), reads=[b_T[2]], writes=[b_junk])
        c.op("dve", lambda: nc.vector.tensor_tensor(out=r, in0=r, in1=h2, op=ALU.subtract),
             reads=[b_T[2], b_junk], writes=[b_T[2]])
        c.op("dve", lambda: nc.vector.tensor_copy(out=h3, in_=r), reads=[b_T[2]], writes=[b_T[2]])
        return [(h1, b_junk), (h2, b_junk), (h3, b_T[2])]

    c.op("pool", lambda: nc.gpsimd.memset(ident[:], 0.0), writes=[b_ident])
    c.op("pool", lambda: nc.gpsimd.affine_select(out=ident[:], in_=ident[:], pattern=[[-1, 128]],
                                                  compare_op=ALU.not_equal, fill=1.0, base=0,
                                                  channel_multiplier=1), reads=[b_ident], writes=[b_ident])
    c.op("pool", lambda: nc.gpsimd.tensor_copy(out=identb[:], in_=ident[:]), reads=[b_ident], writes=[b_identb])
    c.op("pool", lambda: nc.gpsimd.memset(trimf[:], 0.0), writes=[b_trimf])
    c.op("pool", lambda: nc.gpsimd.affine_select(out=trimf[:], in_=trimf[:], pattern=[[1, 128]],
                                                  compare_op=ALU.is_ge, fill=NEG, base=0,
                                                  channel_multiplier=-1), reads=[b_trimf], writes=[b_trimf])
    c.op("pool", lambda: nc.gpsimd.tensor_copy(out=trim[:], in_=trimf[:]), reads=[b_trimf], writes=[b_trim])
    c.op("pool", lambda: nc.gpsimd.memset(ones_f[:], 1.0), writes=[b_ones])
    c.op("pool", lambda: nc.gpsimd.memset(gsrc[:], 0.0), writes=[b_gsrc])
    for i in range(NSLOT):
        c.op("pool", lambda: nc.gpsimd.memset(wslot[i][:], 0.0), writes=[b_ws[i]])
    c.op("pool", lambda: nc.gpsimd.memset(tric[:], 1.0), writes=[b_tric])
    c.op("pool", lambda: nc.gpsimd.affine_select(out=tric[:], in_=tric[:], pattern=[[1, 128]],
                                                  compare_op=ALU.is_ge, fill=0.0, base=0,
                                                  channel_multiplier=-1), reads=[b_tric], writes=[b_tric])
    c.op("pool", lambda: nc.gpsimd.memset(vst[:].rearrange("p s h e -> p (s h e)"), 1.0), writes=[b_vst])
    for i in range(2):
        c.op("pool", lambda: nc.gpsimd.memset(ktile[i][:].rearrange("p a b -> p (a b)"), 1.0), writes=[b_kt[i]])
    c.op("pool", lambda: nc.gpsimd.tensor_copy(out=tricb[:], in_=tric[:]), reads=[b_tric], writes=[b_tricb])
    c.op("pool", lambda: nc.gpsimd.memset(onesb[:], 1.0), writes=[b_onesb])
    c.op("sp", lambda: nc.sync.dma_start(out=maddb[:], in_=maddb_in[:, :]), writes=[b_maddb], dma="c0")
    c.op("sp", lambda: nc.sync.dma_start(out=selt[:], in_=sel_in[:, :]), writes=[b_sel], dma="c1")
    sel = selt[:, 0:4]
    c.op("sp", lambda: nc.sync.dma_start(out=selh[:], in_=selh_in[:, :]), writes=[b_selh], dma="c2")
    c.op("pool", lambda: nc.gpsimd.tensor_copy(out=selhb[:], in_=selh[:]), reads=[b_selh], writes=[b_selhb])

    wfull = {}
    b_wf = {}

    def cast_weights():
        for key, src in wsrc.items():
            rows, cols = src.shape
            wf = dt("wf_" + key, [rows, cols], BF16).ap()
            bf_ = Buf("wf_" + key)
            for r0 in range(0, rows, 128):
                r1 = min(rows, r0 + 128)
                c.op("pool", lambda: nc.gpsimd.dma_start(out=wf[r0:r1, :], in_=src[r0:r1, :], max_dma_last_dim=4096),
                     writes=[bf_], dma="wc_" + key)
            wfull[key] = wf
            b_wf[key] = bf_

    def wcols(key, c0, ncols, kp=128):
        wf = wfull[key]
        v, b = wload(wf[:, c0:c0 + ncols].rearrange("(k p) n -> p k n", p=kp), [kp, wf.shape[0] // kp, ncols])
        return v, b

    def wl_dep(key):
        return [b_wf[key]]

    _wload_plain = wload

    def wload_dep(dram_view, shape, deps):
        i = st["ws"]; st["ws"] = (i + 1) % NSLOT
        n = int(np.prod(shape[1:]))
        v = wslot[i][0:shape[0], 0:n]
        vfull = wslot[i][:, 0:n]
        if len(shape) == 3:
            v = v.rearrange("p (a b) -> p a b", a=shape[1])
            vfull = vfull.rearrange("p (a b) -> p a b", a=shape[1])
        c.op("sp", lambda: nc.sync.dma_start(out=v, in_=dram_view), reads=deps, writes=[b_ws[i]],
             dma="ws%d" % i)
        return vfull, b_ws[i]

    def wblock(key, r0, nrows, c0, ncols, kp=128):
        wf = wfull[key]
        view = wf[r0:r0 + nrows, c0:c0 + ncols].rearrange("(k p) n -> p k n", p=kp)
        return wload_dep(view, [kp, nrows // kp, ncols], [b_wf[key]])

    def load_layer_consts(L):
        par = L % 2
        c.op("sp", lambda: nc.sync.dma_start(out=gsrc[0:48, :], in_=ng_by_L[L].rearrange("g (c p) -> (g c) p", p=128)),
             writes=[b_gsrc], dma="gsrc")
        if par == 1:
            c.op("sp", lambda: nc.sync.dma_start(out=gsrc[48:72, :], in_=cw_in.rearrange("k (c p) -> (k c) p", p=128)),
                 writes=[b_gsrc], dma="gsrc2")
        elif has_pre:
            c.op("sp", lambda: nc.sync.dma_start(out=bfb[:, 0, :], in_=bf_in.partition_broadcast(128)),
                 writes=[b_bfb], dma="bfb")
            for s_ in range(1, 4):
                c.op("pool", lambda: nc.gpsimd.tensor_copy(out=bfb[:, s_, :], in_=bfb[:, 0, :]), reads=[b_bfb], writes=[b_bfb])
        bk = nbank()
        c.op("pe", lambda: nc.tensor.transpose(out=ps[bk][:, 0:128], in_=gsrc[:, :], identity=ident[:]),
             reads=[b_gsrc, b_ident], writes=[b_ps[bk]])
        c.op("act", lambda: nc.scalar.copy(out=gTT[par][:], in_=ps[bk][:, 0:128]), reads=[b_ps[bk]], writes=[b_gTT[par]])

    def load_gbc(L, gi):
        i = st["g"]; st["g"] = 1 - i
        c.op("sp", lambda: nc.sync.dma_start(out=gbc[i][:], in_=ng_by_L[L][gi, :].partition_broadcast(128)),
             writes=[b_gbc[i]], dma="gbc%d" % i)
        return gbc[i], b_gbc[i]

    def load_x(m, src, src_bufs):
        import os as _o
        if _o.environ.get("KDBG") == "xnowar":
            c.op("sp", lambda: nc.sync.dma_start(out=xt[:], in_=src[m * TS:(m + 1) * TS, :].rearrange("(s p) d -> p s d", p=128)),
                 reads=src_bufs, writes=[], dma="xt")
            return
        c.op("sp", lambda: nc.sync.dma_start(out=xt[:], in_=src[m * TS:(m + 1) * TS, :].rearrange("(s p) d -> p s d", p=128)),
             reads=src_bufs, writes=b_xt, dma="xt")

    def store_x(m, dst, dst_bufs, halo):
        c.op("pool", lambda: nc.gpsimd.dma_start(out=dst[m * TS:(m + 1) * TS, :].rearrange("(s p) d -> p s d", p=128), in_=xt[:]),
             reads=b_xt, writes=dst_bufs, dma="xst")
        if halo:
            c.op("pool", lambda: nc.gpsimd.dma_start(out=xhloc[2 * m:2 * m + 2, :], in_=xt[126:128, 3, :]),
                 reads=[b_xt[3]], writes=[b_xhloc], dma="xhst")

    def rstd_from_ss(n, col0=0):
        c.op("act", lambda: nc.scalar.activation(out=rstd[:, col0:col0 + n], in_=ss[:, col0:col0 + n], func=AF.Sqrt,
                                                 scale=1.0 / D, bias=EPS), reads=[b_ss], writes=[b_rstd])
        c.op("dve", lambda: nc.vector.reciprocal(out=rstd[:, col0:col0 + n], in_=rstd[:, col0:col0 + n]),
             reads=[b_rstd], writes=[b_rstd])

    def norm_to_T(gi, L):
        for s in range(4):
            c.op("act", lambda: nc.scalar.activation(out=junk[:], in_=xt[:, s, :], func=AF.Square,
                                                     accum_out=ss[:, s:s + 1]),
                 reads=[b_xt[s]], writes=[b_junk, b_ss])
        rstd_from_ss(4)
        for s in range(4):
            ti = s % 2
            c.op("act", lambda: nc.scalar.activation(out=Tt[ti][:], in_=xt[:, s, :], func=AF.Copy,
                                                     scale=rstd[:, s:s + 1]),
                 reads=[b_xt[s], b_rstd], writes=[b_T[ti]])
            for half in range(2):
                bk = nbank()
                for kk in range(4):
                    k = half * 4 + kk
                    c.op("pe", lambda: nc.tensor.transpose(out=ps[bk][:, kk * 128:(kk + 1) * 128],
                                                           in_=Tt[ti][:, k * 128:(k + 1) * 128], identity=ident[:]),
                         reads=[b_T[ti], b_ident], writes=[b_ps[bk]])
                for kk in range(4):
                    k = half * 4 + kk
                    c.op("dve", lambda: nc.vector.tensor_scalar(out=hnT[:, k, s * 128:(s + 1) * 128],
                                                                in0=ps[bk][:, kk * 128:(kk + 1) * 128],
                                                                scalar1=gTT[L % 2][:, gi * 16 + k:gi * 16 + k + 1], scalar2=None, op0=ALU.mult),
                         reads=[b_ps[bk], b_gTT[L % 2]], writes=[b_hnT])

    def res_norm(s, srcs, src_bufs, gtile, gbuf):
        for h2 in range(2):
            c.op("act", lambda: nc.scalar.activation(out=junk[:, 0:TS], in_=srcs[h2], func=AF.Square,
                                                     accum_out=ss[:, 4 + h2:5 + h2]),
                 reads=[src_bufs[h2]], writes=[b_junk, b_ss])
        c.op("dve", lambda: nc.vector.tensor_add(out=ss[:, 6:7], in0=ss[:, 4:5], in1=ss[:, 5:6]),
             reads=[b_ss], writes=[b_ss])
        rstd_from_ss(1, 6)
        for h2 in range(2):
            c.op("dve", lambda: nc.vector.scalar_tensor_tensor(out=Tt[2][:, h2 * TS:(h2 + 1) * TS], in0=srcs[h2],
                                                               scalar=rstd[:, 6:7], in1=gtile[:, h2 * TS:(h2 + 1) * TS],
                                                               op0=ALU.mult, op1=ALU.mult),
                 reads=[src_bufs[h2], b_rstd, gbuf], writes=[b_T[2]])
        c.op("pool", lambda: nc.gpsimd.tensor_tensor(out=xt[:, s, :], in0=xt[:, s, :], in1=Tt[2][:], op=ALU.add),
             reads=[b_xt[s], b_T[2]], writes=[b_xt[s]])

    def stage_attn_pre(L, m):
        key = "ain%d" % L
        norm_to_T(0, L)
        if stop == "S0a":
            raise StopBuild()
        nst = 0
        for which in range(2):
            for cb in range(2):
                W, bW = wblock(key, 0, D, which * D + cb * 512, 512)
                for hq in range(2):
                    si = nst % 2; nst += 1
                    for p2 in range(2):
                        pr = hq * 2 + p2
                        bk = nbank()
                        for k in range(8):
                            c.op("pe", lambda: nc.tensor.matmul(ps[bk][:, :], lhsT=W[:, k, pr * 128:(pr + 1) * 128],
                                                                rhs=hnT[:, k, :], start=(k == 0), stop=(k == 7)),
                                 reads=[bW, b_hnT], writes=[b_ps[bk]])
                        if which == 0:
                            c.op("act", lambda: nc.scalar.activation(out=qkst[si][:, p2, :], in_=ps[bk][:, :],
                                                                     func=AF.Copy, scale=0.125),
                                 reads=[b_ps[bk]], writes=[b_qkst[si]])
                        else:
                            c.op("dve", lambda: nc.vector.tensor_copy(out=qkst[si][:, p2, :], in_=ps[bk][:, :]),
                                 reads=[b_ps[bk]], writes=[b_qkst[si]])
                    dstt = qbuf if which == 0 else kloc
                    dbuf = [b_qbuf[m]] if which == 0 else [b_kloc]
                    r0 = m * D + cb * 512 + hq * 256
                    c.op("pool", lambda: nc.gpsimd.dma_start(out=dstt[r0:r0 + 256, :].rearrange("(i q) t -> q i t", q=128),
                                                             in_=qkst[si][:]),
                         reads=[b_qkst[si]], writes=dbuf, dma="qkst%d" % si)
        if stop == "S0b":
            raise StopBuild()
        for cb in range(2):
            W, bW = wblock(key, 0, D, 2 * D + cb * 512, 512)
            for s in range(4):
                bk = nbank()
                for k in range(8):
                    c.op("pe", lambda: nc.tensor.matmul(ps[bk][:, :], lhsT=hnT[:, k, s * 128:(s + 1) * 128],
                                                        rhs=W[:, k, :], start=(k == 0), stop=(k == 7)),
                         reads=[bW, b_hnT], writes=[b_ps[bk]])
                c.op("dve", lambda: nc.vector.tensor_copy(out=vst[:, s, cb * 8:(cb + 1) * 8, 0:64],
                                                          in_=ps[bk][:, :].rearrange("p (h d) -> p h d", d=64)),
                     reads=[b_ps[bk]], writes=[b_vst])
        c.op("pool", lambda: nc.gpsimd.dma_start(out=vloc[m * TS:(m + 1) * TS, :].rearrange("(s p) e -> p s e", p=128),
                                                 in_=vst[:].rearrange("p s h e -> p s (h e)")),
             reads=[b_vst], writes=[b_vloc], dma="vst")
        if stop == "S0c":
            raise StopBuild()
        W, bW = wblock(key, 0, D, 3 * D + H - 512, 512)
        bk = nbank()
        for s in range(4):
            for k in range(8):
                c.op("pe", lambda: nc.tensor.matmul(ps[bk][:, s * H:(s + 1) * H], lhsT=hnT[:, k, s * 128:(s + 1) * 128],
                                                    rhs=W[:, k, 512 - H:512], start=(k == 0), stop=(k == 7)),
                     reads=[bW, b_hnT], writes=[b_ps[bk]])
        f2 = lambda t: t[:].rearrange("p s h -> p (s h)")
        c.op("dve", lambda: nc.vector.tensor_tensor(out=f2(fz), in0=ps[bk][:, 0:4 * H], in1=f2(bfb), op=ALU.add),
             reads=[b_ps[bk], b_bfb], writes=[b_fz])
        c.op("act", lambda: nc.scalar.activation(out=f2(fa), in_=f2(fz), func=AF.Abs),
             reads=[b_fz], writes=[b_fa])
        c.op("act", lambda: nc.scalar.activation(out=f2(fa), in_=f2(fa), func=AF.Exp, scale=-1.0),
             reads=[b_fa], writes=[b_fa])
        c.op("act", lambda: nc.scalar.activation(out=f2(fa), in_=f2(fa), func=AF.Ln, bias=1.0),
             reads=[b_fa], writes=[b_fa])
        c.op("dve", lambda: nc.vector.tensor_scalar_min(out=f2(fl), in0=f2(fz), scalar1=0.0),
             reads=[b_fz], writes=[b_fl])
        c.op("dve", lambda: nc.vector.tensor_sub(out=f2(fl), in0=f2(fl), in1=f2(fa)),
             reads=[b_fl, b_fa], writes=[b_fl])
        c.op("pool", lambda: nc.gpsimd.dma_start(out=lfloc[:, m * 64:(m + 1) * 64], in_=f2(fl)),
             reads=[b_fl], writes=[b_lfloc], dma="lfst")

    def attn_exchange(L):
        if stop == "ex0a":
            raise StopBuild()
        NC16 = NB * H
        af = arena[:].rearrange("p a b -> p (a b)").bitcast(F32)
        Sv = af[:, 0:NC16]
        Pv = af[:, 2048:2048 + NC16]
        LFv = af[:, 4096:4096 + NC16]
        negf = negc[:].rearrange("p b h -> p (b h)")
        c.op("sp", lambda: nc.sync.dma_start(out=LFv.rearrange("p (m r x) -> p m r x", r=4, x=64),
                                             in_=lfall.rearrange("(r p) (m x) -> p m r x", p=128, x=64)),
             reads=[b_lfall], writes=b_ar, dma="lfl")
        if stop == "ex0b":
            raise StopBuild()
        nch = (NC16 + TS - 1) // TS
        for ch in range(nch):
            w = min(TS, NC16 - ch * TS)
            bA = nbank(); bB = nbank()
            terms = split3(LFv[:, ch * TS:ch * TS + w], list(b_ar), w)
            for ti_, (hh_, hb_) in enumerate(terms):
                c.op("pe", lambda: nc.tensor.matmul(ps[bA][:, 0:w], lhsT=tricb[:, :], rhs=hh_,
                                                    start=(ti_ == 0), stop=(ti_ == 2)), reads=[hb_, b_tricb], writes=[b_ps[bA]])
            for ti_, (hh_, hb_) in enumerate(terms):
                c.op("pe", lambda: nc.tensor.matmul(ps[bB][:, 0:w], lhsT=onesb[:, :], rhs=hh_,
                                                    start=(ti_ == 0), stop=(ti_ == 2)), reads=[hb_, b_onesb], writes=[b_ps[bB]])
            c.op("act", lambda: nc.scalar.copy(out=negf[:, ch * TS:ch * TS + w], in_=ps[bA][:, 0:w]),
                 reads=[b_ps[bA]], writes=[b_negc])
            c.op("dve", lambda: nc.vector.tensor_copy(out=Sv[:, ch * TS:ch * TS + w], in_=ps[bB][:, 0:w]),
                 reads=[b_ps[bB]], writes=b_ar)
        if stop == "ex0b2":
            raise StopBuild()
        c.op("dve", lambda: nc.vector.memset(Pv[:, 0:H], 0.0), writes=b_ar)
        for b_ in range(1, NB):
            c.op("dve", lambda: nc.vector.tensor_add(out=Pv[:, b_ * H:(b_ + 1) * H], in0=Pv[:, (b_ - 1) * H:b_ * H],
                                                     in1=Sv[:, (b_ - 1) * H:b_ * H]), reads=b_ar[0:1], writes=b_ar[0:1])
        if stop == "ex0b3":
            raise StopBuild()
        c.op("dve", lambda: nc.vector.scalar_tensor_tensor(out=negf, in0=negf, scalar=-1.0, in1=Pv, op0=ALU.mult,
                                                           op1=ALU.subtract), reads=b_ar + [b_negc], writes=[b_negc])
        if stop == "ex0c":
            raise StopBuild()
        nv = negc[:].rearrange("p (m r k) h -> p m r (k h)", r=4, k=4)
        for r in range(4):
            if r == 0:
                c.op("dve", lambda: nc.vector.tensor_scalar(out=nown[:], in0=nv[:, :, 0, :], scalar1=sel[:, 0:1],
                                                            scalar2=None, op0=ALU.mult),
                     reads=[b_negc, b_sel], writes=[b_nown])
            else:
                c.op("dve", lambda: nc.vector.scalar_tensor_tensor(out=nown[:], in0=nv[:, :, r, :], scalar=sel[:, r:r + 1],
                                                                   in1=nown[:], op0=ALU.mult, op1=ALU.add),
                     reads=[b_negc, b_sel, b_nown], writes=[b_nown])
        ncol = max(128, NT * 64)
        nownf = nown[:].rearrange("p m x -> p (m x)")
        hb = junk[:, 0:ncol]
        hi32 = Tt[0][:, 0:ncol]
        lo32 = Tt[1][:, 0:ncol]
        c.op("pool", lambda: nc.gpsimd.memset(Tt[0][:], 0.0), writes=[b_T[0]])
        c.op("pool", lambda: nc.gpsimd.memset(Tt[1][:], 0.0), writes=[b_T[1]])
        c.op("dve", lambda: nc.vector.tensor_scalar(out=hb[:, 0:NT * 64], in0=nownf, scalar1=-1.0, scalar2=None, op0=ALU.mult),
             reads=[b_nown], writes=[b_junk])
        c.op("dve", lambda: nc.vector.tensor_copy(out=hi32[:, 0:NT * 64], in_=hb[:, 0:NT * 64]), reads=[b_junk, b_T[0]], writes=[b_T[0]])
        c.op("dve", lambda: nc.vector.scalar_tensor_tensor(out=lo32[:, 0:NT * 64], in0=nownf, scalar=-1.0, in1=hi32[:, 0:NT * 64],
                                                           op0=ALU.mult, op1=ALU.subtract),
             reads=[b_nown, b_T[0], b_T[1]], writes=[b_T[1]])
        for x, src, sbuf_ in ((0, hi32, b_T[0]), (1, lo32, b_T[1])):
            for g8 in range(ncol // 128):
                bk = nbank()
                c.op("pe", lambda: nc.tensor.transpose(out=ps[bk][:, 0:128], in_=src[:, g8 * 128:(g8 + 1) * 128], identity=ident[:]),
                     reads=[sbuf_, b_ident], writes=[b_ps[bk]])
                ci = (x * (ncol // 128) + g8) % 2
                c.op("act", lambda: nc.scalar.copy(out=cTt[ci][:], in_=ps[bk][:, 0:128]), reads=[b_ps[bk]], writes=[b_cT[ci]])
                for b8 in range(8):
                    blk = g8 * 8 + b8
                    mm = blk // 4; s_ = blk % 4
                    if mm >= NT:
                        continue
                    c.op("pool", lambda: nc.gpsimd.dma_start(out=cqd[x, mm, :, s_ * 128:(s_ + 1) * 128],
                                                             in_=cTt[ci][b8 * H:(b8 + 1) * H, :]),
                         reads=[b_cT[ci]], writes=[b_cqd[mm]], dma="cT%d" % ci)

    def stage_attn(L, m):
        QT = arena
        qb = b_ar[0:16]
        import os as _os
        _skip = _os.environ.get("KSKIP", "").split(",")
        if "q" not in _skip:
            c.op("sp", lambda: nc.sync.dma_start(out=arena[0:64, 0:16, :],
                                                 in_=qbuf[m * D:(m + 1) * D, :].rearrange("(h d) t -> d h t", d=64)),
                 reads=[b_qbuf[m]], writes=qb, dma="qld")
        if "aug" not in _skip:
            c.op("sp", lambda: nc.sync.dma_start(out=arena[64:65, 0:16, :], in_=cqd[0, m:m + 1, :, :]),
                 reads=[b_cqd[m]], writes=qb, dma="qld1")
            c.op("sp", lambda: nc.sync.dma_start(out=arena[65:66, 0:16, :], in_=cqd[1, m:m + 1, :, :]),
                 reads=[b_cqd[m]], writes=qb, dma="qld2")
        if "ms" not in _skip:
            c.op("pool", lambda: nc.gpsimd.memset(junk[:], 0.0), writes=[b_junk])
            c.op("pool", lambda: nc.gpsimd.memset(arena[64:128, 16:32, :], 0.0), writes=b_ar[16:32])
        if stop == "B0":
            raise StopBuild()
        nt = 4 * m + 3
        steps = [("all", t) for t in range(nt)] + [("own", m)]
        bmv = biasm[:].rearrange("p a (t k) -> p a t k", k=4)
        ncv = negc[:].rearrange("p (t k) h -> p t k h", k=4)
        mdv = maddb[:, 0:NT * NB].rearrange("p (m t k) -> p m t k", m=NT, k=4)
        kvi = 0
        pti = 0
        for hg in range(4):
            for hh in range(4):
                h = hg * 4 + hh
                c.op("dve", lambda: nc.vector.tensor_tensor(out=bmv[:, hh, 0:nt, :], in0=ncv[:, 0:nt, :, h],
                                                            in1=mdv[:, m, 0:nt, :], op=ALU.add),
                     reads=[b_negc, b_maddb], writes=[b_biasm])
            if stop == "B1":
                raise StopBuild()
            ob = [4, 5, 6, 7]
            items = []
            for si, (kind, t) in enumerate(steps):
                for hh in range(4):
                    for kb in range(4):
                        items.append((si, kind, t, hh, kb))
            LOOK = 2
            st_ki = {}
            pend = {}
            for idx in range(len(items) + LOOK):
                if idx < len(items):
                    si, kind, t, hh, kb = items[idx]
                    h = hg * 4 + hh
                    if hh == 0 and kb == 0:
                        ki = kvi % 2; kvi += 1
                        st_ki[si] = ki
                        if kind == "all":
                            krow = (((t // 4) * 4 + hg) * 4 + (t % 4)) * 256
                            ksrc, kbuf, vbuf = kall, b_kall, b_vall
                            vview = vall.rearrange("(m k r p) e -> m r p k e", k=4, r=4, p=128)[t // 4, t % 4]
                        else:
                            krow = m * D + hg * 256
                            ksrc, kbuf, vbuf = kloc, b_kloc, b_vloc
                            vview = vloc[m * TS:(m + 1) * TS, :].rearrange("(k p) e -> p k e", p=128)
                        c.op("sp", lambda: nc.sync.dma_start(out=ktile[ki][0:64, :, :],
                                                             in_=ksrc[krow:krow + 256, :].rearrange("(h d) t -> d h t", d=64)),
                             reads=[kbuf], writes=[b_kt[ki]], dma="kt%d" % ki)
                        c.op("sp", lambda: nc.sync.dma_start(out=vtile[ki][:], in_=vview[:, :, hg * 260:(hg + 1) * 260]),
                             reads=[vbuf], writes=[b_vt[ki]], dma="vt%d" % ki)
                    ki = st_ki[si]
                    sbk = nbank(0, 4)
                    q0 = kb * 128 if kind == "own" else 0
                    if kind == "all":
                        c.op("pe", lambda: nc.tensor.matmul(ps[sbk][:, :], lhsT=ktile[ki][0:66, hh, kb * 128:(kb + 1) * 128],
                                                            rhs=arena[0:66, h, :], start=True, stop=True),
                             reads=[b_kt[ki], b_ar[h]], writes=[b_ps[sbk]])
                        bias_ap = biasm[:, hh, t * 4 + kb:t * 4 + kb + 1]
                        brd = b_biasm
                    else:
                        c.op("pe", lambda: nc.tensor.matmul(ps[sbk][:, q0:TS], lhsT=ktile[ki][0:66, hh, kb * 128:(kb + 1) * 128],
                                                            rhs=arena[0:66, h, q0:TS], start=True, stop=False),
                             reads=[b_kt[ki], b_ar[h]], writes=[b_ps[sbk]])
                        c.op("pe", lambda: nc.tensor.matmul(ps[sbk][:, q0:q0 + 128], lhsT=identb[:], rhs=trim[:],
                                                            start=False, stop=True),
                             reads=[b_identb, b_trim], writes=[b_ps[sbk]])
                        bias_ap = nown[:, m, kb * H + h:kb * H + h + 1]
                        brd = b_nown
                    pend[idx] = (sbk, q0, bias_ap, brd, ki)
                j = idx - LOOK
                if j >= 0:
                    si, kind, t, hh, kb = items[j]
                    sbk, q0, bias_ap, brd, ki = pend.pop(j)
                    pi = pti % NPT; pti += 1
                    c.op("act", lambda: nc.scalar.activation(out=ptile[pi][:, q0:TS], in_=ps[sbk][:, q0:TS], func=AF.Exp,
                                                             bias=bias_ap, scale=1.0),
                         reads=[b_ps[sbk], brd], writes=[b_pt[pi]])
                    first = (si == 0 and kb == 0)
                    last = (kind == "own" and kb == 3)
                    c.op("pe", lambda: nc.tensor.matmul(ps[ob[hh]][0:65, q0:TS], lhsT=vtile[ki][:, kb, hh * 65:(hh + 1) * 65],
                                                        rhs=ptile[pi][:, q0:TS], start=first, stop=last),
                         reads=[b_vt[ki], b_pt[pi]], writes=[b_ps[ob[hh]]])
            if stop == "B4":
                raise StopBuild()
            for hh in range(4):
                h = hg * 4 + hh
                c.op("act", lambda: nc.scalar.copy(out=OTs[0:65, :], in_=ps[ob[hh]][0:65, :]),
                     reads=[b_ps[ob[hh]]], writes=[b_OTs])
                c.op("dve", lambda: nc.vector.reciprocal(out=rinv[64:65, :], in_=OTs[64:65, :]),
                     reads=[b_OTs], writes=[b_rinv])
                c.op("dve", lambda: nc.vector.tensor_copy(out=junk[64:65, 0:TS], in_=rinv[64:65, :]),
                     reads=[b_rinv], writes=[b_junk])
                c.op("dve", lambda: nc.vector.tensor_tensor(out=rinv[64:65, :], in0=rinv[64:65, :], in1=junk[64:65, 0:TS],
                                                            op=ALU.subtract), reads=[b_rinv, b_junk], writes=[b_rinv])
                c.op("dve", lambda: nc.vector.tensor_copy(out=junk[64:65, TS:2 * TS], in_=rinv[64:65, :]),
                     reads=[b_rinv], writes=[b_junk])
                bk = nbank(0, 4)
                c.op("pe", lambda: nc.tensor.matmul(ps[bk][:, :], lhsT=onesb[:, :], rhs=junk[:, 0:TS],
                                                    start=True, stop=False),
                     reads=[b_onesb, b_junk], writes=[b_ps[bk]])
                c.op("pe", lambda: nc.tensor.matmul(ps[bk][:, :], lhsT=onesb[:, :], rhs=junk[:, TS:2 * TS],
                                                    start=False, stop=True),
                     reads=[b_onesb, b_junk], writes=[b_ps[bk]])
                c.op("dve", lambda: nc.vector.tensor_tensor(out=arena[0:64, 16 + h, :], in0=OTs[0:64, :], in1=ps[bk][0:64, :],
                                                            op=ALU.mult),
                     reads=[b_OTs, b_ps[bk]], writes=[b_ar[16 + h]])

    def conv_halo_prep(L):
        key = "cin%d" % L
        n2 = 2 * NT
        c.op("pool", lambda: nc.gpsimd.memset(Tt[0][:], 0.0), writes=[b_T[0]])
        c.op("pool", lambda: nc.gpsimd.memset(Tt[1][:], 0.0), writes=[b_T[1]])
        c.op("sp", lambda: nc.sync.dma_start(out=Tt[0][0:8 * NT, :], in_=xhall[:, :]), reads=[b_xhall], writes=[b_T[0]], dma="xhl")
        for h2 in range(2):
            bk = nbank()
            terms = split3(Tt[0][:, h2 * TS:(h2 + 1) * TS], [b_T[0]], TS)
            for ti_, (hh_, hb_) in enumerate(terms):
                c.op("pe", lambda: nc.tensor.matmul(ps[bk][:, :], lhsT=selhb[:, :], rhs=hh_,
                                                    start=(ti_ == 0), stop=(ti_ == 2)),
                     reads=[b_selhb, hb_], writes=[b_ps[bk]])
            c.op("dve", lambda: nc.vector.tensor_copy(out=Tt[1][0:n2, h2 * TS:(h2 + 1) * TS], in_=ps[bk][0:n2, :]),
                 reads=[b_ps[bk]], writes=[b_T[1]])
        c.op("act", lambda: nc.scalar.activation(out=junk[0:n2, :], in_=Tt[1][0:n2, :], func=AF.Square,
                                                 accum_out=ss[0:n2, 7:8]), reads=[b_T[1]], writes=[b_junk, b_ss])
        c.op("act", lambda: nc.scalar.activation(out=rstd[0:n2, 7:8], in_=ss[0:n2, 7:8], func=AF.Sqrt,
                                                 scale=1.0 / D, bias=EPS), reads=[b_ss], writes=[b_rstd])
        c.op("dve", lambda: nc.vector.reciprocal(out=rstd[0:n2, 7:8], in_=rstd[0:n2, 7:8]), reads=[b_rstd], writes=[b_rstd])
        c.op("act", lambda: nc.scalar.activation(out=Tt[1][0:n2, :], in_=Tt[1][0:n2, :], func=AF.Copy,
                                                 scale=rstd[0:n2, 7:8]), reads=[b_T[1], b_rstd], writes=[b_T[1]])
        for half in range(2):
            bk = nbank()
            for kk in range(4):
                k = half * 4 + kk
                c.op("pe", lambda: nc.tensor.transpose(out=ps[bk][:, kk * 128:(kk + 1) * 128], in_=Tt[1][:, k * 128:(k + 1) * 128],
                                                       identity=ident[:]),
                     reads=[b_T[1], b_ident], writes=[b_ps[bk]])
            for kk in range(4):
                k = half * 4 + kk
                c.op("dve", lambda: nc.vector.tensor_scalar(out=hnTh[:, k, :], in0=ps[bk][:, kk * 128:kk * 128 + n2],
                                                            scalar1=gTT[L % 2][:, k:k + 1], scalar2=None, op0=ALU.mult),
                     reads=[b_ps[bk], b_gTT[L % 2]], writes=[b_hnTh])
        for q in range(2):
            WC, bWC = wblock(key, 0, D, D + q * 512, 512)
            WU, bWU = wblock(key, 0, D, 2 * D + q * 512, 512)
            for fq in range(4):
                fc = q * 4 + fq
                bk = nbank()
                for k in range(8):
                    c.op("pe", lambda: nc.tensor.matmul(ps[bk][:, 0:n2], lhsT=WC[:, k, fq * 128:(fq + 1) * 128],
                                                        rhs=hnTh[:, k, :], start=(k == 0), stop=(k == 7)),
                         reads=[bWC, b_hnTh], writes=[b_ps[bk]])
                c.op("act", lambda: nc.scalar.copy(out=chh[:], in_=ps[bk][:, 0:n2]), reads=[b_ps[bk]], writes=[b_chh])
                bk = nbank()
                for k in range(8):
                    c.op("pe", lambda: nc.tensor.matmul(ps[bk][:, 0:n2], lhsT=WU[:, k, fq * 128:(fq + 1) * 128],
                                                        rhs=hnTh[:, k, :], start=(k == 0), stop=(k == 7)),
                         reads=[bWU, b_hnTh], writes=[b_ps[bk]])
                c.op("dve", lambda: nc.vector.tensor_tensor(out=zh[:, fc, :], in0=ps[bk][:, 0:n2], in1=chh[:], op=ALU.mult),
                     reads=[b_ps[bk], b_chh], writes=[b_zh])

    def stage_conv(L, m):
        key = "cin%d" % L
        norm_to_T(0, L)
        for q in range(2):
            WC, bWC = wblock(key, 0, D, D + q * 512, 512)
            WU, bWU = wblock(key, 0, D, 2 * D + q * 512, 512)
            WB, bWB = wblock(key, 0, D, q * 512, 512)
            for fq in range(4):
                fc = q * 4 + fq
                bk = nbank()
                for k in range(8):
                    c.op("pe", lambda: nc.tensor.matmul(ps[bk][:, :], lhsT=WC[:, k, fq * 128:(fq + 1) * 128],
                                                        rhs=hnT[:, k, :], start=(k == 0), stop=(k == 7)),
                         reads=[bWC, b_hnT], writes=[b_ps[bk]])
                c.op("act", lambda: nc.scalar.copy(out=csb[:], in_=ps[bk][:, :]), reads=[b_ps[bk]], writes=[b_csb])
                bk = nbank()
                for k in range(8):
                    c.op("pe", lambda: nc.tensor.matmul(ps[bk][:, :], lhsT=WU[:, k, fq * 128:(fq + 1) * 128],
                                                        rhs=hnT[:, k, :], start=(k == 0), stop=(k == 7)),
                         reads=[bWU, b_hnT], writes=[b_ps[bk]])
                c.op("dve", lambda: nc.vector.tensor_tensor(out=zt[:, 2:TS + 2], in0=ps[bk][:, :], in1=csb[:], op=ALU.mult),
                     reads=[b_ps[bk], b_csb], writes=[b_zt])
                c.op("pool", lambda: nc.gpsimd.tensor_copy(out=zt[:, 0:2], in_=zh[:, fc, 2 * m:2 * m + 2]),
                     reads=[b_zh], writes=[b_zt])
                c.op("dve", lambda: nc.vector.tensor_scalar(out=zc[:], in0=zt[:, 0:TS], scalar1=gTT[1][:, 48 + fc:48 + fc + 1],
                                                             scalar2=None, op0=ALU.mult),
                     reads=[b_zt, b_gTT[1]], writes=[b_zc])
                for tap in (1, 2):
                    c.op("dve", lambda: nc.vector.scalar_tensor_tensor(out=zc[:], in0=zt[:, tap:TS + tap],
                                                                        scalar=gTT[1][:, 48 + tap * 8 + fc:48 + tap * 8 + fc + 1], in1=zc[:],
                                                                        op0=ALU.mult, op1=ALU.add),
                         reads=[b_zt, b_gTT[1], b_zc], writes=[b_zc])
                bk = nbank()
                for k in range(8):
                    c.op("pe", lambda: nc.tensor.matmul(ps[bk][:, :], lhsT=WB[:, k, fq * 128:(fq + 1) * 128],
                                                        rhs=hnT[:, k, :], start=(k == 0), stop=(k == 7)),
                         reads=[bWB, b_hnT], writes=[b_ps[bk]])
                c.op("dve", lambda: nc.vector.tensor_tensor(out=arena[:, 16 + fc, :], in0=ps[bk][:, :], in1=zc[:], op=ALU.mult),
                     reads=[b_ps[bk], b_zc], writes=[b_ar[16 + fc]])

    def stage_post(L, m):
        attn = (L % 2 == 0)
        okey = ("aout%d" if attn else "cout%d") % L
        g1, bg1 = load_gbc(L, 1)
        Ws = []
        for c2 in range(2):
            if attn:
                Ws.append([wblock(okey, hb * 512, 512, c2 * 512, 512, kp=64) for hb in range(2)])
            else:
                Ws.append([wblock(okey, 0, D, c2 * 512, 512)])
        for s in range(4):
            bks = []
            for c2 in range(2):
                bk = nbank(); bks.append(bk)
                if attn:
                    for h in range(16):
                        W, bW = Ws[c2][h // 8]
                        c.op("pe", lambda: nc.tensor.matmul(ps[bk][:, :], lhsT=arena[:, 16 + h, s * 128:(s + 1) * 128],
                                                            rhs=W[:, h % 8, :], start=(h == 0), stop=(h == 15)),
                             reads=[bW, b_ar[16 + h]], writes=[b_ps[bk]])
                else:
                    W, bW = Ws[c2][0]
                    for k in range(8):
                        c.op("pe", lambda: nc.tensor.matmul(ps[bk][:, :], lhsT=arena[:, 16 + k, s * 128:(s + 1) * 128],
                                                            rhs=W[:, k, :], start=(k == 0), stop=(k == 7)),
                             reads=[bW, b_ar[16 + k]], writes=[b_ps[bk]])
            res_norm(s, [ps[bks[0]][:, :], ps[bks[1]][:, :]], [b_ps[bks[0]], b_ps[bks[1]]], g1, bg1)
        if stop == "C1":
            raise StopBuild()
        g3, bg3 = load_gbc(L, 3)
        norm_to_T(1, L)
        ukey = "up%d" % L
        for ub in range(8):
            W, bW = wblock(ukey, 0, D, ub * 512, 512)
            for fq in range(4):
                ff = ub * 4 + fq
                bk = nbank()
                for k in range(8):
                    c.op("pe", lambda: nc.tensor.matmul(ps[bk][:, :], lhsT=W[:, k, fq * 128:(fq + 1) * 128],
                                                        rhs=hnT[:, k, :], start=(k == 0), stop=(k == 7)),
                         reads=[bW, b_hnT], writes=[b_ps[bk]])
                ti = ff % 2
                c.op("act", lambda: nc.scalar.activation(out=Tt[ti][:, 0:TS], in_=ps[bk][:, :], func=AF.Relu),
                     reads=[b_ps[bk]], writes=[b_T[ti]])
                c.op("pool", lambda: nc.gpsimd.tensor_tensor(out=arena[:, ff, :], in0=Tt[ti][:, 0:TS], in1=Tt[ti][:, 0:TS],
                                                             op=ALU.mult),
                     reads=[b_T[ti]], writes=[b_ar[ff]])
        dkey = "dn%d" % L
        for pc in range(8):
            W, bW = wblock(dkey, pc * 512, 512, 0, D)
            for s in range(4):
                for c2 in range(2):
                    bk = s * 2 + c2
                    for fq in range(4):
                        ff = pc * 4 + fq
                        c.op("pe", lambda: nc.tensor.matmul(ps[bk][:, :], lhsT=arena[:, ff, s * 128:(s + 1) * 128],
                                                            rhs=W[:, fq, c2 * 512:(c2 + 1) * 512],
                                                            start=(ff == 0), stop=(ff == 31)),
                             reads=[bW, b_ar[ff]], writes=[b_ps[bk]])
        for s in range(4):
            res_norm(s, [ps[2 * s][:, :], ps[2 * s + 1][:, :]], [b_ps[2 * s], b_ps[2 * s + 1]], g3, bg3)
        st["bank"] = 0
        if stop == "C2":
            raise StopBuild()
        g5, bg5 = load_gbc(L, 5)
        norm_to_T(2, L)
        c.op("sp", lambda: nc.sync.dma_start(out=ptl[:], in_=p_in[m * TS:(m + 1) * TS, :].rearrange("(s p) e -> p s e", p=128)),
             writes=[b_ptl], dma="ptl")
        for kc in range(2):
            bk = nbank()
            for s in range(4):
                c.op("pe", lambda: nc.tensor.transpose(out=ps[bk][:, s * 128:(s + 1) * 128], in_=ptl[:, s, kc * 128:(kc + 1) * 128],
                                                       identity=ident[:]),
                     reads=[b_ptl, b_ident], writes=[b_ps[bk]])
            c.op("act", lambda: nc.scalar.copy(out=pT[:, kc, :], in_=ps[bk][:, :]), reads=[b_ps[bk]], writes=[b_pT])
        gkey = "pg%d" % L; pkey = "pp%d" % L
        WG = [wblock(gkey, 0, D, c2 * 512, 512) for c2 in range(2)]
        WP, bWP = wblock(pkey, 0, PLE, 0, D)
        for s in range(4):
            for c2 in range(2):
                bg_ = nbank()
                W, bW = WG[c2]
                for k in range(8):
                    c.op("pe", lambda: nc.tensor.matmul(ps[bg_][:, :], lhsT=hnT[:, k, s * 128:(s + 1) * 128], rhs=W[:, k, :],
                                                        start=(k == 0), stop=(k == 7)),
                         reads=[bW, b_hnT], writes=[b_ps[bg_]])
                be_ = nbank()
                for kc in range(2):
                    c.op("pe", lambda: nc.tensor.matmul(ps[be_][:, :], lhsT=pT[:, kc, s * 128:(s + 1) * 128],
                                                        rhs=WP[:, kc, c2 * 512:(c2 + 1) * 512], start=(kc == 0), stop=(kc == 1)),
                         reads=[bWP, b_pT], writes=[b_ps[be_]])
                c.op("act", lambda: nc.scalar.activation(out=Tt[0][:, c2 * TS:(c2 + 1) * TS], in_=ps[bg_][:, :], func=AF.Sigmoid),
                     reads=[b_ps[bg_]], writes=[b_T[0]])
                c.op("dve", lambda: nc.vector.tensor_tensor(out=Tt[1][:, c2 * TS:(c2 + 1) * TS], in0=ps[be_][:, :],
                                                            in1=Tt[0][:, c2 * TS:(c2 + 1) * TS], op=ALU.mult),
                     reads=[b_ps[be_], b_T[0]], writes=[b_T[1]])
            res_norm(s, [Tt[1][:, 0:TS], Tt[1][:, TS:2 * TS]], [b_T[1], b_T[1]], g5, bg5)

    def finish():
        for en in ("sp", "pool", "act", "dve", "pe"):
            for k in c.sems:
                if c.cnt[k] > 0:
                    c.engs[en].wait_ge(c.sems[k], c.cnt[k])
        es.close()
        return nc, c

    cast_weights()
    b_xin = []
    try:
        if phase == "pre":
            load_layer_consts(0)
            for m in range(NT):
                load_x(m, x_in, b_xin)
                stage_attn_pre(0, m)
        elif phase == "attn":
            load_layer_consts(LA)
            attn_exchange(LA)
            for m in range(NT):
                load_x(m, x_in, b_xin)
                stage_attn(LA, m)
                stage_post(LA, m)
                store_x(m, out_d, [], halo=True)
        else:
            load_layer_consts(LC)
            if phase == "convpre":
                load_layer_consts(2)
            conv_halo_prep(LC)
            for m in range(NT):
                load_x(m, x_in, b_xin)
                stage_conv(LC, m)
                stage_post(LC, m)
                if phase == "convpre":
                    stage_attn_pre(2, m)
                store_x(m, out_d, [], halo=False)
    except StopBuild:
        pass
    return finish()


def host_consts(NT, j):
    NB = 16 * NT
    madd = np.zeros((NT, NB), np.float32)
    for m in range(NT):
        for kbi in range(NB):
            t = kbi // 4
            if not (t < 4 * m + j):
                madd[m, kbi] = NEG
    sel = np.zeros((128, 4), np.float32); sel[:, j] = 1.0
    maddb = np.broadcast_to(madd.reshape(1, NT * NB), (128, NT * NB)).astype(np.float32).copy()
    sel = np.concatenate([sel, np.zeros((128, 12), np.float32)], axis=1)
    selh = np.zeros((128, 128), np.float32)
    for m in range(NT):
        g = 4 * m + j
        if g >= 1:
            r2 = (g - 1) % 4; m2 = (g - 1) // 4
            for i in range(2):
                selh[r2 * 2 * NT + m2 * 2 + i, m * 2 + i] = 1.0
    return maddb, sel, selh


def _shard_idx(NT, j):
    return np.concatenate([np.arange((4 * m + j) * TS, (4 * m + j + 1) * TS) for m in range(NT)])


_CACHE = {}


def _run(phase, NT, in_maps):
    key = (phase, NT)
    if key not in _CACHE:
        _CACHE[key] = build(NT, phase)[0]
    res = run_bass_kernel_spmd(_CACHE[key], in_maps, core_ids=list(range(8)))
    return res.results


def _gather_kv(res, NT):
    out = []
    for b in range(2):
        rs = [res[4 * b + r] for r in range(4)]
        kall = np.concatenate([np.asarray(rs[r]["kloc"])[ci * 256:(ci + 1) * 256] for ci in range(NT * 4) for r in range(4)], axis=0)
        vall = np.concatenate([np.asarray(rs[r]["vloc"])[ci * 128:(ci + 1) * 128] for ci in range(NT * 4) for r in range(4)], axis=0)
        lfall = np.concatenate([np.asarray(rs[r]["lfloc"]) for r in range(4)], axis=0)
        out.append((kall, vall, lfall))
    return out


def kernel(**inputs):
    f = lambda k: np.asarray(inputs[k], np.float32)
    x = f("x"); p = f("p"); ng = f("norm_g"); bfg = f("b_forget"); cw = f("conv_w")
    wai = f("w_attn_in"); wao = f("w_attn_out"); wci = f("w_conv_in"); wco = f("w_conv_out")
    wup = f("w_mlp_up"); wdn = f("w_mlp_down"); wpp = f("w_ple_proj"); wpg = f("w_ple_gate")
    B, S, _ = x.shape
    NT = S // (4 * TS)
    cores = list(range(8))
    idx = [_shard_idx(NT, c % 4) for c in cores]
    consts = []
    for c in cores:
        maddb, sel, selh = host_consts(NT, c % 4)
        consts.append({"maddb": maddb, "sel": sel, "selh": selh})
    xs = [np.ascontiguousarray(x[c // 4][idx[c]]) for c in cores]
    psh = [[np.ascontiguousarray(p[L, c // 4][idx[c]]) for c in cores] for L in range(4)]

    def mlp_w(L):
        return {"w_up": wup[L], "w_dn": wdn[L], "w_pp": wpp[L], "w_pg": wpg[L]}

    r = _run("pre", NT, [dict(consts[c], x=xs[c], norm_g_pre=ng[0], b_forget=bfg[0], w_ain=wai[0]) for c in cores])
    for L in (0, 2):
        a = L // 2
        kv = _gather_kv(r, NT)
        im = []
        for c in cores:
            kall, vall, lfall = kv[c // 4]
            d = dict(consts[c], x=xs[c], p=psh[L][c], norm_g=ng[L], w_out=wao[a], qbuf=np.asarray(r[c]["qbuf"]),
                     kloc=np.asarray(r[c]["kloc"]), vloc=np.asarray(r[c]["vloc"]), kall=kall, vall=vall, lfall=lfall)
            d.update(mlp_w(L))
            im.append(d)
        r2 = _run("attn", NT, im)
        xs = [np.asarray(r2[c]["out"]) for c in cores]
        xh = [np.concatenate([np.asarray(r2[4 * b + rr]["xhloc"]) for rr in range(4)], axis=0) for b in range(2)]
        Lc = L + 1
        im = []
        for c in cores:
            d = dict(consts[c], x=xs[c], p=psh[Lc][c], norm_g=ng[Lc], conv_w=cw[a], w_cin=wci[a], w_out=wco[a],
                     xhall=xh[c // 4])
            d.update(mlp_w(Lc))
            if L == 0:
                d.update(norm_g_pre=ng[2], b_forget=bfg[1], w_ain=wai[1])
            im.append(d)
        r = _run("convpre" if L == 0 else "conv", NT, im)
        xs = [np.asarray(r[c]["out"]) for c in cores]
    out = np.zeros((B, S, D), np.float32)
    for c in cores:
        out[c // 4][idx[c]] = xs[c]
    return out
```
